# Optimizing a Trainium2 kernel written in Bass

```python
import jax, jax.numpy as jnp
from jax import lax
import numpy as np

D_MODEL = 1024
BATCH = 4
SEQ = 8192
DEPTH = 1

D_MIX = D_MODEL
RET_HEADS = 4
RET_HEAD_DIM = 128
RET_W = RET_HEADS * RET_HEAD_DIM
GLA_HEADS = 4
GLA_KEY_DIM = 64
GLA_VALUE_DIM = 128
GLA_KW = GLA_HEADS * GLA_KEY_DIM
GLA_VW = GLA_HEADS * GLA_VALUE_DIM
GLA_GATE_RANK = 16
GLA_GATE_NORMALIZER = 16.0
RET_CHUNK = 128
GLA_CHUNK = 64
D_FF = 2816
CONV_WIDTH = 3
ROPE_BASE = 10000.0
EPS = 1e-6
SPLITS = (RET_W, RET_W, RET_W, RET_W, GLA_KW, GLA_KW, GLA_VW, GLA_VW, GLA_GATE_RANK)
D_IN_PROJ = RET_W * 4 + GLA_KW * 2 + GLA_VW * 2 + GLA_GATE_RANK

kernel_name = "hybrid_retention_gla_convffn"


def rms_norm(x, w):
    xf = x.astype(jnp.float32)
    y = xf * lax.rsqrt(jnp.mean(xf * xf, axis=-1, keepdims=True) + EPS)
    return (y * w.astype(jnp.float32)).astype(x.dtype)


def rotary(t, positions):
    half = t.shape[-1] // 2
    inv_freq = ROPE_BASE ** (-jnp.arange(half, dtype=jnp.float32) / half)
    ang = positions.astype(jnp.float32)[..., None] * inv_freq
    cos = jnp.cos(ang)[:, :, None, :]
    sin = jnp.sin(ang)[:, :, None, :]
    tf = t.astype(jnp.float32)
    t1, t2 = tf[..., :half], tf[..., half:]
    return jnp.concatenate([t1 * cos - t2 * sin, t2 * cos + t1 * sin], axis=-1)


def to_chunks(t, c):
    b, s, h, d = t.shape
    return t.reshape(b, s // c, c, h, d).transpose(1, 0, 3, 2, 4)


def from_chunks(o):
    n, b, h, c, d = o.shape
    return o.transpose(1, 0, 3, 2, 4).reshape(b, n * c, h, d)


def retention_chunkwise(q, k, v):
    b, s, h, dk = q.shape
    dv = v.shape[-1]
    c = RET_CHUNK
    log_gamma = jnp.log(1.0 - 2.0 ** (-5.0 - jnp.arange(h, dtype=jnp.float32)))
    j = jnp.arange(c, dtype=jnp.float32)
    diff = j[:, None] - j[None, :]
    intra = jnp.where(diff >= 0, jnp.exp(log_gamma[:, None, None] * jnp.maximum(diff, 0.0)), 0.0)
    q_dec = jnp.exp(log_gamma[:, None] * (j + 1.0))[:, :, None]
    k_dec = jnp.exp(log_gamma[:, None] * (c - 1.0 - j))[:, :, None]
    chunk_dec = jnp.exp(log_gamma * c)[:, None, None]

    def step(state, inp):
        qc, kc, vc = inp
        scores = jnp.einsum('bhjd,bhld->bhjl', qc, kc) * intra
        o = jnp.einsum('bhjl,bhle->bhje', scores, vc) \
            + jnp.einsum('bhjd,bhde->bhje', qc * q_dec, state)
        state = state * chunk_dec + jnp.einsum('bhld,bhle->bhde', kc * k_dec, vc)
        return state, o

    state0 = jnp.zeros((b, h, dk, dv), jnp.float32)
    _, o = lax.scan(step, state0, (to_chunks(q, c), to_chunks(k, c), to_chunks(v, c)))
    return from_chunks(o)


def gla_chunkwise(q, k, v, log_g):
    b, s, h, dk = q.shape
    dv = v.shape[-1]
    c = GLA_CHUNK
    mask = jnp.tril(jnp.ones((c, c), dtype=bool))[:, :, None]

    def step(state, inp):
        qc, kc, vc, gc = inp
        cum = jnp.cumsum(gc, axis=2)
        pair = cum[:, :, :, None, :] - cum[:, :, None, :, :]
        decay = jnp.exp(jnp.where(mask, pair, -jnp.inf))
        attn = jnp.einsum('bhjd,bhld,bhjld->bhjl', qc, kc, decay)
        o = jnp.einsum('bhjl,bhle->bhje', attn, vc) \
            + jnp.einsum('bhjd,bhde->bhje', qc * jnp.exp(cum), state)
        last = cum[:, :, -1:, :]
        state = state * jnp.swapaxes(jnp.exp(last), -1, -2) \
            + jnp.einsum('bhld,bhle->bhde', kc * jnp.exp(last - cum), vc)
        return state, o

    state0 = jnp.zeros((b, h, dk, dv), jnp.float32)
    _, o = lax.scan(step, state0, (to_chunks(q, c), to_chunks(k, c), to_chunks(v, c), to_chunks(log_g, c)))
    return from_chunks(o)


def causal_depthwise_conv(u, w, bias):
    ch = u.shape[-1]
    y = lax.conv_general_dilated(u, w[:, None, :], window_strides=(1,),
                                 padding=[(CONV_WIDTH - 1, 0)],
                                 dimension_numbers=('NWC', 'WIO', 'NWC'),
                                 feature_group_count=ch)
    return y + bias


def setup_inputs(seed: int = 0) -> dict:
    key = jax.random.key(seed)
    ks = jax.random.split(key, 16)
    f32 = jnp.float32
    L = DEPTH
    nrm = lambda k, shape, scale: jax.random.normal(k, shape, f32) * scale
    return {
        "x": jax.random.normal(ks[0], (BATCH, SEQ, D_MODEL), f32),
        "positions": jnp.broadcast_to(jnp.arange(SEQ, dtype=jnp.int32), (BATCH, SEQ)),
        "norm1_w": 1.0 + nrm(ks[1], (L, D_MODEL), 0.02),
        "w_in": nrm(ks[2], (L, D_MODEL, D_IN_PROJ), D_MODEL ** -0.5),
        "ret_norm_w": 1.0 + nrm(ks[3], (L, RET_W), 0.02),
        "ret_norm_b": nrm(ks[4], (L, RET_W), 0.02),
        "gla_gate_w2": nrm(ks[5], (L, GLA_GATE_RANK, GLA_KW), GLA_GATE_RANK ** -0.5),
        "gla_gate_b": nrm(ks[6], (L, GLA_KW), 0.02),
        "gla_norm_w": 1.0 + nrm(ks[7], (L, GLA_VW), 0.02),
        "w_out": nrm(ks[8], (L, D_MIX, D_MODEL), D_MIX ** -0.5),
        "norm2_w": 1.0 + nrm(ks[9], (L, D_MODEL), 0.02),
        "ffn_w_up": nrm(ks[10], (L, D_MODEL, 2 * D_FF), D_MODEL ** -0.5),
        "ffn_conv_w": nrm(ks[11], (L, CONV_WIDTH, 2 * D_FF), CONV_WIDTH ** -0.5),
        "ffn_conv_b": nrm(ks[12], (L, 2 * D_FF), 0.02),
        "ffn_w_down": nrm(ks[13], (L, D_FF, D_MODEL), D_FF ** -0.5),
        "final_norm_w": 1.0 + nrm(ks[14], (D_MODEL,), 0.02),
    }


def reference(x, positions, norm1_w, w_in, ret_norm_w, ret_norm_b, gla_gate_w2, gla_gate_b,
              gla_norm_w, w_out, norm2_w, ffn_w_up, ffn_conv_w, ffn_conv_b, ffn_w_down, final_norm_w):
    b, s, _ = x.shape
    offsets = []
    acc = 0
    for width in SPLITS[:-1]:
        acc += width
        offsets.append(acc)
    heads = lambda t, n: t.reshape(b, s, n, -1)
    h = x
    for layer in range(DEPTH):
        xn = rms_norm(h, norm1_w[layer])
        proj = xn @ w_in[layer]
        rq, rk, rv, rg, gq, gk, gv, gg, g_low = jnp.split(proj, offsets, axis=-1)

        q_r = rotary(heads(rq, RET_HEADS), positions)
        k_r = rotary(heads(rk, RET_HEADS), positions) * (RET_HEAD_DIM ** -0.5)
        o_r = retention_chunkwise(q_r, k_r, heads(rv, RET_HEADS).astype(jnp.float32))
        mu = jnp.mean(o_r, axis=-1, keepdims=True)
        var = jnp.mean(jnp.square(o_r - mu), axis=-1, keepdims=True)
        o_r = ((o_r - mu) * lax.rsqrt(var + EPS)).reshape(b, s, RET_W)
        o_r = (o_r * ret_norm_w[layer].astype(jnp.float32) + ret_norm_b[layer].astype(jnp.float32)) \
            * jax.nn.silu(rg.astype(jnp.float32))

        gate_logits = (g_low @ gla_gate_w2[layer] + gla_gate_b[layer]).astype(jnp.float32)
        log_g = jax.nn.log_sigmoid(gate_logits) / GLA_GATE_NORMALIZER
        q_g = heads(gq, GLA_HEADS).astype(jnp.float32) * (GLA_KEY_DIM ** -0.5)
        k_g = heads(gk, GLA_HEADS).astype(jnp.float32)
        v_g = heads(gv, GLA_HEADS).astype(jnp.float32)
        o_g = gla_chunkwise(q_g, k_g, v_g, heads(log_g, GLA_HEADS))
        o_g = (o_g * lax.rsqrt(jnp.mean(o_g * o_g, axis=-1, keepdims=True) + EPS)).reshape(b, s, GLA_VW)
        o_g = o_g * gla_norm_w[layer].astype(jnp.float32) * jax.nn.silu(gg.astype(jnp.float32))

        mixed = jnp.concatenate([o_r, o_g], axis=-1).astype(h.dtype)
        h = h + mixed @ w_out[layer]

        xn2 = rms_norm(h, norm2_w[layer])
        u = causal_depthwise_conv(xn2 @ ffn_w_up[layer], ffn_conv_w[layer], ffn_conv_b[layer])
        u_gate, u_val = jnp.split(u, 2, axis=-1)
        h = h + (jax.nn.silu(u_gate) * u_val) @ ffn_w_down[layer]
    return rms_norm(h, final_norm_w)
```

```python
import math
import os
from contextlib import ExitStack

import numpy as np
import concourse.bass as bass
import concourse.mybir as mybir
from concourse.bass_utils import run_bass_kernel_spmd

F32 = mybir.dt.float32
BF16 = mybir.dt.bfloat16
I32 = mybir.dt.int32
ALU = mybir.AluOpType
AF = mybir.ActivationFunctionType

D = 1024
DIN = 3600
DFF = 2816
NG = 22
EPS = 1e-6
NT = 4
GAM = [1.0 - 2.0 ** (-5.0 - h) for h in range(4)]
C_RQ, C_RK, C_RV, C_RG, C_GQ, C_GK, C_GV, C_GG, C_GL = 0, 512, 1024, 1536, 2048, 2304, 2560, 3072, 3584


class Prog:
    ENG = ("pe", "act", "dve", "pool", "sp")

    def __init__(self, nc, stack):
        self.nc = nc
        self.stack = stack
        self.streams = {e: [] for e in self.ENG}
        self.sem = {e: stack.enter_context(nc.semaphore("s_" + e)) for e in self.ENG if e != "sp"}
        self.cnt = {e: 0 for e in self.ENG}
        self.waited = {e: {} for e in self.ENG}
        self.last_w = {}
        self.readers = {}
        self.dma_sems = {}
        self.alias = {}
        self.regions = []
        self.strict_same = bool(int(os.environ.get("STRICT_SAME", "1")))
        self.rec = None
        self.filler = None
        self.in_fill = False
        self.n_fill = 0
        self.fill_min = float(os.environ.get("FILL_MIN", "0.7"))
        self.fill_frac = float(os.environ.get("FILL_FRAC", "0.7"))
        self.fill_max = int(os.environ.get("FILL_MAX", "16"))
        self.t_w = {}
        self.t_r = {}
        self.t_eng = {e: 0.0 for e in self.ENG}

    def region(self, key, start, end):
        for (k2, s2, e2) in self.regions:
            if start < e2 and s2 < end:
                self.alias.setdefault(key, []).append(k2)
                self.alias.setdefault(k2, []).append(key)
        self.regions.append((key, start, end))

    def new_dma_sem(self, name, total=None):
        s = self.stack.enter_context(self.nc.semaphore(name))
        self.dma_sems[name] = [s, 0, None if total is None else 16 * total]
        return name

    def _deps(self, eng, reads, writes):
        deps = []
        for r in reads:
            for k in [r] + self.alias.get(r, []):
                t = self.last_w.get(k)
                if t is not None:
                    deps.append((t, True))
        for w in writes:
            for k in [w] + self.alias.get(w, []):
                t = self.last_w.get(k)
                if t is not None:
                    deps.append((t, False))
                for t in self.readers.get(k, ()):
                    deps.append((t, False))
        need = {}
        for (semname, val, semobj, teng), is_raw in deps:
            if teng == eng:
                if eng in ("pe", "sp"):
                    continue
                if not is_raw and not self.strict_same:
                    continue
            if self.waited[eng].get(semname, 0) >= val:
                continue
            if need.get(semname, (0, None))[0] < val:
                need[semname] = (val, semobj)
        for semname, (val, semobj) in need.items():
            self.waited[eng][semname] = val
            self.streams[eng].append(("wait", semobj, val))

    def _commit(self, token, reads, writes):
        for w in writes:
            self.last_w[w] = token
            self.readers[w] = []
        for r in reads:
            if r not in writes:
                self.readers.setdefault(r, []).append(token)

    def record(self):
        self.rec = []
        self.cuts = []

    def cut(self):
        if self.rec is not None:
            self.cuts.append(len(self.rec))

    def stop_chunks(self):
        r, cuts = self.rec, self.cuts
        self.rec = None
        out, prev = [], 0
        for c in cuts + [len(r)]:
            out.append(r[prev:c])
            prev = c
        return out

    def stop(self):
        r, self.rec = self.rec, None
        return r

    DEFC = {"pe": 0.22, "act": 0.6, "dve": 0.5, "pool": 1.2, "sp": 0.1}

    def _est_start(self, eng, reads, writes):
        t = self.t_eng[eng]

        def lat(e2):
            if e2 != eng:
                return 0.2
            return 0.0 if eng in ("pe", "sp") else 0.08
        for r in reads:
            for k in [r] + self.alias.get(r, []):
                v = self.t_w.get(k)
                if v is not None:
                    t = max(t, v[0] + lat(v[1]))
        for w in writes:
            for k in [w] + self.alias.get(w, []):
                v = self.t_w.get(k)
                if v is not None:
                    t = max(t, v[0] + lat(v[1]))
                v = self.t_r.get(k)
                if v is not None:
                    t = max(t, v[0] + lat(v[1]))
        return t

    def _est_commit(self, eng, reads, writes, cost, is_dma=False):
        t0 = self._est_start(eng, reads, writes)
        if os.environ.get("KTRACE") and float(os.environ["KTRACE"].split(",")[0]) <= t0 <= float(os.environ["KTRACE"].split(",")[1]):
            if t0 - self.t_eng[eng] > float(os.environ.get("KSTALL", "0")):
                blk = None
                for k in list(reads) + list(writes):
                    for kk in [k] + self.alias.get(k, []):
                        for dct in (self.t_w, self.t_r):
                            v = dct.get(kk)
                            if v is not None and (blk is None or v[0] > blk[0]):
                                blk = (v[0], v[1], kk)
                print("  %8.2f %-4s %5.2f stall=%.2f blocked_by=%s w=%s" % (t0, eng, cost, t0 - self.t_eng[eng], blk, list(writes)[:2]))
        if is_dma:
            self.t_eng[eng] = t0 + 0.1
            t1 = t0 + cost
        else:
            t1 = t0 + cost
            self.t_eng[eng] = t1
        we = "dma" if is_dma else eng
        for w in writes:
            self.t_w[w] = (t1, we)
            self.t_r.pop(w, None)
        for r in reads:
            if r not in writes:
                if r not in self.t_r or self.t_r[r][0] < t1:
                    self.t_r[r] = (t1, we)

    def _rec_info(self, r):
        if r[0] in ("op", "fill"):
            return r[1], r[3], r[4]
        return r[1], r[5], r[6]

    def play(self, recs):
        for r in recs:
            if r[0] == "op":
                self.op(*r[1:])
            elif r[0] == "fill":
                if self.filler is not None:
                    self.in_fill = True
                    for _ in range(r[6]):
                        self.filler()
                    self.in_fill = False
                    self.n_fill += r[6]
            else:
                self.dma(*r[1:])

    def fill(self, n):
        if self.rec is not None:
            self.rec.append(("fill", "pe", None, [], [], 0.0, n))

    def play_stages(self, stages):
        state = [[0, [0] * len(st[0])] if st else None for st in stages]
        while True:
            best = None
            for si, st in enumerate(stages):
                if state[si] is None:
                    continue
                ph, ptrs = state[si]
                while ph < len(st) and all(ptrs[ci] >= len(st[ph][ci]) for ci in range(len(st[ph]))):
                    ph += 1
                    if ph < len(st):
                        ptrs = [0] * len(st[ph])
                if ph >= len(st):
                    state[si] = None
                    continue
                state[si] = [ph, ptrs]
                for ci, ch in enumerate(st[ph]):
                    if ptrs[ci] < len(ch):
                        e, r, w = self._rec_info(ch[ptrs[ci]])
                        t = self._est_start(e, r, w)
                        if best is None or t < best[0]:
                            best = (t, si, ci)
            if best is None:
                break
            _, si, ci = best
            ph, ptrs = state[si]
            self.play([stages[si][ph][ci][ptrs[ci]]])
            ptrs[ci] += 1

    def play_merged(self, a, b):
        ia = ib = 0
        while ia < len(a) or ib < len(b):
            if ib >= len(b):
                pick_a = True
            elif ia >= len(a):
                pick_a = False
            else:
                ea, ra, wa = self._rec_info(a[ia])
                eb, rb, wb = self._rec_info(b[ib])
                pick_a = self._est_start(ea, ra, wa) <= self._est_start(eb, rb, wb)
            if pick_a:
                self.play([a[ia]]); ia += 1
            else:
                self.play([b[ib]]); ib += 1

    def op(self, eng, fn, reads=(), writes=(), c=None):
        if self.rec is not None:
            self.rec.append(("op", eng, fn, list(reads), list(writes), c))
            return
        if eng == "pe" and self.filler is not None and not self.in_fill:
            stall = self._est_start(eng, reads, writes) - self.t_eng["pe"]
            if stall >= self.fill_min:
                n = min(self.fill_max, int(self.fill_frac * stall / 0.22))
                self.in_fill = True
                for _ in range(n):
                    self.filler()
                self.in_fill = False
                self.n_fill += n
        self._est_commit(eng, reads, writes, c if c is not None else self.DEFC[eng])
        self._deps(eng, reads, writes)
        self.cnt[eng] += 1
        token = ("s_" + eng, self.cnt[eng], self.sem[eng], eng)
        self.streams[eng].append(("op", fn, self.sem[eng], 1))
        self._commit(token, reads, writes)

    def dma(self, eng, semname, out, in_, reads=(), writes=(), slow=False):
        if self.rec is not None:
            self.rec.append(("dma", eng, semname, out, in_, list(reads), list(writes), slow))
            return
        self._est_commit(eng, reads, writes, 2.5, is_dma=True)
        self._deps(eng, reads, writes)
        ent = self.dma_sems[semname]
        ent[1] += 16
        token = (semname, ent[2] if ent[2] is not None else ent[1], ent[0], "dma")
        self.streams[eng].append(("op", lambda e, o=out, i=in_, s=slow: e.dma_start(o, i, allow_slow_non_contiguous=s), ent[0], 16))
        self._commit(token, reads, writes)

    def emit(self):
        nc = self.nc
        for name, ent in self.dma_sems.items():
            assert ent[2] is None or ent[2] == ent[1], (name, ent[1], ent[2])
        engmap = {"pe": "tensor", "act": "scalar", "dve": "vector", "pool": "gpsimd", "sp": "sync"}
        with nc.Block() as block:
            for e in self.ENG:
                items = self.streams[e]

                def body(engine, items=items):
                    for it in items:
                        if it[0] == "wait":
                            engine.wait_ge(it[1], it[2])
                        else:
                            it[1](engine).then_inc(it[2], it[3])
                getattr(block, engmap[e])(body)


def interleave(a, b):
    out = []
    na, nb = len(a), len(b)
    ia = ib = 0
    while ia < na or ib < nb:
        if ib >= nb or (ia < na and ia * nb <= ib * na):
            out.append(a[ia]); ia += 1
        else:
            out.append(b[ib]); ib += 1
    return out


def build(NPRE, NSUP):
    NV = NPRE + 1 + NT * NSUP
    NMAIN = NT * NSUP
    nc = bass.Bass("TRN2", target_bir_lowering=False)
    din = lambda name, shape, dt=F32: nc.dram_tensor(name, shape, dt, kind="ExternalInput").ap()
    xs_d = din("xs", [NV * 128, D])
    pos_d = din("pos", [NV, 128], I32)
    n1w_d = din("norm1_w", [D]); n2w_d = din("norm2_w", [D]); fnw_d = din("final_norm_w", [1, D])
    win_d = din("w_in", [D, DIN]); wout_d = din("w_out", [D, D])
    wup_d = din("ffn_w_up", [D, 2 * DFF]); wdn_d = din("ffn_w_down", [DFF, D])
    rnw_d = din("ret_norm_w", [1, 512]); rnb_d = din("ret_norm_b", [1, 512]); gnw_d = din("gla_norm_w", [1, 512])
    gw2_d = din("gla_gate_w2", [16, 256]); gb_d = din("gla_gate_b", [1, 256])
    cw_d = din("ffn_conv_w", [3, 2 * DFF]); cb_d = din("ffn_conv_b", [2 * DFF])
    ident_d = din("c_ident", [128, 128]); maskr_d = din("c_maskr", [128, 128]); maskg_d = din("c_maskg", [128, 128])
    triu_d = din("c_triu", [128, 128]); tril_d = din("c_tril", [128, 128])
    rot_d = din("c_rot", [2, 128]); cvec_d = din("c_vec", [128, 16])
    y_d = nc.dram_tensor("y", [NMAIN * 128, D], F32, kind="ExternalOutput").ap()
    wup_s = nc.dram_tensor("wup_s", [D, 2 * DFF], BF16, kind="Internal").ap()
    wdn_s = nc.dram_tensor("wdn_s", [DFF, D], BF16, kind="Internal").ap()

    with ExitStack() as st:
        P = Prog(nc, st)
        sb_bytes = [0]

        def T(name, shape, dt):
            n = 1
            for d_ in shape[1:]:
                n *= d_
            sb_bytes[0] += n * (2 if dt == BF16 else 4)
            return st.enter_context(nc.sbuf_tensor(name, shape, dt))
        bank = [st.enter_context(nc.psum_tensor("bank%d" % i, [128, 512], F32)) for i in range(8)]
        bkey = lambda i: "bank%d" % i
        bf = lambda i: bank[i][:].bitcast(BF16).rearrange("p (k c) -> p k c", c=128)
        f3 = lambda i: bank[i][:].rearrange("p (h c) -> p h c", c=128)

        win = T("win", [128, 8, DIN], BF16)
        wout = T("wout", [128, 8, D], BF16)
        NWU, NWD = 2, 2
        wu = [T("wu%d" % i, [128, 2, 8, 256], BF16) for i in range(NWU)]
        fnw = T("fnw", [128, D], F32)
        rnw = T("rnw", [128, 512], F32); rnb = T("rnb", [128, 512], F32); gnw = T("gnw", [128, 512], F32)
        n1wT = T("n1wT", [128, 8], F32); n2wT = T("n2wT", [128, 8], F32)
        identf = T("identf", [128, 128], F32); ident = T("ident", [128, 128], BF16)
        maskr = T("maskr", [128, 128], F32); maskg = T("maskg", [128, 128], F32)
        triu = T("triu", [128, 128], F32); tril = T("tril", [128, 128], F32)
        rotc = T("rotc", [128, 2, 128], F32)
        cvec = T("cvec", [128, 16], F32)
        gw2f = T("gw2f", [17, 256], F32); gw2b = T("gw2b", [17, 256], BF16)
        cwt = T("cwt", [128, 4, 2 * NG], F32)
        posi = T("posi", [NV, 128], I32); posr = T("posr", [NV, 128], F32); posf = T("posf", [128, NV], F32)
        xb = [T("xb%d" % i, [128, D], F32) for i in range(3)]
        hb = [T("hb%d" % i, [128, D], F32) for i in range(NT)]
        hnT = T("hnT", [128, 8, NT * 128], BF16)
        Sr = T("Sr", [128, 4, 128], F32); Srb = T("Srb", [128, 4, 128], BF16)
        Sg = T("Sg", [64, 4, 128], F32); Sgb = T("Sgb", [64, 4, 128], BF16)
        carry = T("carry", [128, 2 * NG, 2], F32)
        sm = T("sm", [128, 64], F32)
        glT = T("glT", [17, 128], BF16)

        KEEP_BYTES = 2 * (4 * 1024 + 2 * 2048 + 3 * 512)
        UNI_BYTES = KEEP_BYTES + 35328
        uni = T("uni", [128, UNI_BYTES // 4], F32)
        off = {"keep": 0, "mix": KEEP_BYTES, "ffn": KEEP_BYTES}

        def carve(kind, key, ncols, dt):
            nbytes = ncols * (2 if dt == BF16 else 4)
            s = off[kind]
            off[kind] += nbytes
            assert off[kind] <= (KEEP_BYTES if kind == "keep" else UNI_BYTES), (kind, key, off[kind])
            P.region(key, s, s + nbytes)
            coff[key] = s
            v = uni[:, s // 4:(s + nbytes) // 4]
            return v.bitcast(BF16) if dt == BF16 else v

        coff = {}

        def carve_at(key, s, ncols, dt):
            nbytes = ncols * (2 if dt == BF16 else 4)
            P.region(key, s, s + nbytes)
            v = uni[:, s // 4:(s + nbytes) // 4]
            return v.bitcast(BF16) if dt == BF16 else v

        AO = []
        for S in range(2):
            d = {}
            for nm in ("qr", "kr", "vr", "vg"):
                d[nm] = carve("keep", "%s_%d" % (nm, S), 512, BF16)
            for nm in ("sgr", "sgg"):
                d[nm] = carve("keep", "%s_%d" % (nm, S), 512, F32)
            for nm in ("qg", "kg", "kh"):
                d[nm] = carve("keep", "%s_%d" % (nm, S), 256, BF16)
            AO.append(d)
        junk = carve("mix", "junk", D, BF16)
        xn = carve("mix", "xn", D, BF16)
        xT_main = carve("mix", "xT", D, BF16).rearrange("p (k c) -> p k c", c=128)
        rot = [carve("mix", "rot%d" % i, 256, F32).rearrange("p (h c) -> p h c", c=64) for i in range(4)]
        glb = carve("mix", "glb", 16, BF16)
        e1 = carve("mix", "e1", 256, F32); nls = carve("mix", "nls", 256, F32)
        eq = carve("mix", "eq", 256, F32); ek = carve("mix", "ek", 256, F32); erc_main = carve("mix", "erc", 256, F32)
        qkT = carve("mix", "qkT", 1024, BF16).rearrange("p (k c) -> p k c", c=128)
        scm = carve("mix", "scm", 512, BF16).rearrange("p (h c) -> p h c", c=128)
        dsc = carve("mix", "dsc", 512, F32).rearrange("p (h c) -> p h c", c=128)
        qkgT = carve("mix", "qkgT", 1024, BF16).rearrange("p (k c) -> p k c", c=128)
        nr = carve("mix", "nr", 512, F32); ng = carve("mix", "ng", 512, F32)
        mixed = carve("mix", "mixed", D, BF16)
        mT = carve("mix", "mT", D, BF16).rearrange("p (k c) -> p k c", c=128)
        hn = carve("mix", "hn", D, BF16)
        cs_main = carve("mix", "cs", 128, F32); ckf = carve("mix", "ckf", 128, F32); ckm = carve("mix", "ckm", 128, F32)
        cki = carve("mix", "cki", 128, F32).bitcast(I32)
        xT_alt1 = carve_at("xT_alt1", coff["mixed"], D, BF16).rearrange("p (k c) -> p k c", c=128)
        xT_alt2 = carve_at("xT_alt2", coff["mT"], D, BF16).rearrange("p (k c) -> p k c", c=128)
        erc_alt = carve_at("erc_alt", coff["nr"], 256, F32)
        cs_alt1 = carve_at("cs_alt1", coff["nr"] + 1024, 128, F32)
        cs_alt2 = carve_at("cs_alt2", coff["nr"] + 1536, 128, F32)
        NACC = 6
        acc = [carve("ffn", "acc%d" % i, NT * 128, F32) for i in range(NACC)]
        actT = [carve("ffn", "actT%d" % g, NT * 128, BF16) for g in range(NG)]
        dumr = T("dumr", [128, 512], BF16)
        corr = T("corr", [128, 2 * NG, 2], F32)
        ctmp = T("ctmp", [128, 2 * NG], F32)

        if os.environ.get("KDEBUG"):
            print("SBUF bytes/partition:", sb_bytes[0])
        for i in range(3):
            P.new_dma_sem("xld%d" % i)
        P.new_dma_sem("su", total=10)
        P.new_dma_sem("su2", total=11)
        for i in range(12):
            P.new_dma_sem("stg%d" % i)
        P.new_dma_sem("sucB", total=26)
        P.new_dma_sem("sucf", total=38)
        su = lambda out, in_, key, slow=False: P.dma("sp", "su", out, in_, writes=[key], slow=slow)
        su2 = lambda out, in_, key, slow=False: P.dma("sp", "su2", out, in_, writes=[key], slow=slow)
        su(identf[:], ident_d, "identf"); su(cvec[:], cvec_d, "cvec")
        su(rotc[:, 0, :], rot_d[0:1, :].partition_broadcast(128), "rotc0"); su(rotc[:, 1, :], rot_d[1:2, :].partition_broadcast(128), "rotc1")
        su(posi[:], pos_d, "posi")
        su(n1wT[:], n1w_d.rearrange("(k p) -> p k", p=128), "n1wT", slow=True)
        su(gw2f[0:16, :], gw2_d, "gw2fa"); su(gw2f[16:17, :], gb_d, "gw2fb")
        su(triu[:], triu_d, "triu"); su(tril[:], tril_d, "tril")
        P.dma("sp", "xld0", xb[0][:], xs_d[0:128, :], writes=["xb0"])
        su2(maskr[:], maskr_d, "maskr"); su2(maskg[:], maskg_d, "maskg")
        su2(fnw[:], fnw_d.partition_broadcast(128), "fnw")
        su2(rnw[:], rnw_d.partition_broadcast(128), "rnw"); su2(rnb[:], rnb_d.partition_broadcast(128), "rnb"); su2(gnw[:], gnw_d.partition_broadcast(128), "gnw")
        su2(n2wT[:], n2w_d.rearrange("(k p) -> p k", p=128), "n2wT", slow=True)
        for i in range(3):
            su2(cwt[:, i, :], cw_d[i].rearrange("(g p) -> p g", p=128), "cwt%d" % i, slow=True)
        su2(cwt[:, 3, :], cb_d.rearrange("(g p) -> p g", p=128), "cwt3", slow=True)
        win_v = win_d.rearrange("(k p) c -> p k c", p=128)
        wout_v = wout_d.rearrange("(k p) c -> p k c", p=128)
        RA = [(C_RK, C_RG), (C_GK, C_GG), (C_GL, DIN)]
        RB = [(C_RQ, C_RK), (C_RG, C_GK), (C_GG, C_GL)]
        WINA = ["winA%d_%d" % (k, i) for k in range(8) for i in range(3)]
        WINB = ["winB%d_%d" % (k, i) for k in range(8) for i in range(3)]
        WIN = WINA + WINB
        WOUT = ["wout0", "wout1"]
        WUPS = ["wup_s%d" % k for k in range(16)]
        WDNS = ["wdn_s%d" % k for k in range(22)]
        stage = [uni[:, KEEP_BYTES // 4 + sl * 1808:KEEP_BYTES // 4 + (sl + 1) * 1808] for sl in range(4)]
        for sl in range(4):
            P.region("stage%d" % sl, KEEP_BYTES + sl * 1808 * 4, KEEP_BYTES + (sl + 1) * 1808 * 4)
        soff = [0, 1024, 1792]
        for sl in range(4):
            for i, (c0, c1) in enumerate([(C_RK, C_RG), (C_GK, C_GG), (C_GL, DIN)]):
                b0 = KEEP_BYTES + sl * 1808 * 4 + soff[i] * 4
                P.region("stage%d_%d" % (sl, i), b0, b0 + (c1 - c0) * 4)
        for k in range(8):
            sl = k % 4
            for i, (c0, c1) in enumerate(RA):
                n_ = c1 - c0
                P.dma("sp", "stg%d" % (sl * 3 + i), stage[sl][:, soff[i]:soff[i] + n_], win_v[:, k, c0:c1], writes=["stage%d_%d" % (sl, i)])
            for i, (c0, c1) in enumerate(RA):
                n_ = c1 - c0
                eng_ = ("dve", "act", "pool")[i]
                if eng_ == "act":
                    P.op("act", lambda e, k=k, sl=sl, i=i, c0=c0, c1=c1, n_=n_: e.copy(win[:, k, c0:c1], stage[sl][:, soff[i]:soff[i] + n_]),
                         ["stage%d_%d" % (sl, i)], ["winA%d_%d" % (k, i)], 0.9)
                else:
                    P.op(eng_, lambda e, k=k, sl=sl, i=i, c0=c0, c1=c1, n_=n_: e.tensor_copy(win[:, k, c0:c1], stage[sl][:, soff[i]:soff[i] + n_]),
                         ["stage%d_%d" % (sl, i)], ["winA%d_%d" % (k, i)], 1.0)
        bulk = []
        for k in range(16):
            bulk.append(("sucf", wup_s[64 * k:64 * k + 64, :], wup_d[64 * k:64 * k + 64, :], WUPS[k]))
        for k in range(8):
            for i, (c0, c1) in enumerate(RB):
                bulk.append(("sucB", win[:, k, c0:c1], win_v[:, k, c0:c1], "winB%d_%d" % (k, i)))
        for k in range(2):
            bulk.append(("sucB", wout[:, 4 * k:4 * k + 4, :], wout_v[:, 4 * k:4 * k + 4, :], WOUT[k]))
        for k in range(22):
            bulk.append(("sucf", wdn_s[128 * k:128 * k + 128, :], wdn_d[128 * k:128 * k + 128, :], WDNS[k]))

        def issue_bulk(n):
            for _ in range(min(n, len(bulk))):
                sem_, o_, i_, key_ = bulk.pop(0)
                P.dma("pool", sem_, o_, i_, writes=[key_])
        wup_sv = wup_s.rearrange("(k p) c -> p k c", p=128)
        wdn_sv = wdn_s.rearrange("(k p) c -> p k c", p=128)

        dve = lambda fn, r, w, c=None: P.op("dve", fn, r, w, c)
        act = lambda fn, r, w, c=None: P.op("act", fn, r, w, c)
        pool = lambda fn, r, w, c=None: P.op("pool", fn, r, w, c)
        pe = lambda fn, r, w, c=None: P.op("pe", fn, r, w, c)

        dve(lambda e: e.tensor_copy(ident[:], identf[:]), ["identf"], ["ident"])
        dve(lambda e: e.tensor_copy(gw2b[:], gw2f[:]), ["gw2fa", "gw2fb"], ["gw2b"])
        dve(lambda e: e.tensor_copy(posr[:], posi[:]), ["posi"], ["posr"])
        pe(lambda e: e.matmul(bank[3][:, 0:NV], posr[:], identf[0:NV, 0:NV], start=True, stop=True), ["posr", "identf"], [bkey(3)])
        dve(lambda e: e.tensor_copy(posf[:], bank[3][:, 0:NV]), [bkey(3)], ["posf"])
        dve(lambda e: e.memset(Sr[:], 0.0), [], ["Sr"])
        dve(lambda e: e.memset(Sg[:], 0.0), [], ["Sg"])
        dve(lambda e: e.memset(Srb[:], 0.0), [], ["Srb"])
        dve(lambda e: e.memset(Sgb[:], 0.0), [], ["Sgb"])
        dve(lambda e: e.memset(carry[:], 0.0), [], ["carry%d" % c for c in range(2 * NG)])
        dve(lambda e: e.memset(glT[:], 1.0), [], ["glT"])
        dve(lambda e: e.memset(dumr[:], 0.5), [], ["dumr"])
        fill_i = [0]

        def filler():
            b_ = (5, 7)[fill_i[0] % 2]
            fill_i[0] += 1
            P.op("pe", lambda e, b_=b_: e.matmul(bank[b_][:], ident[:], dumr[:], start=True, stop=True), ["ident", "dumr"], ["fillbank%d" % b_], 0.22)

        for i in range(NT):
            P.new_dma_sem("yst%d" % i)
        for i in range(NWU):
            P.new_dma_sem("wul%d_0" % i)
            P.new_dma_sem("wul%d_1" % i)

        def load_x(t):
            P.dma("sp", "xld%d" % (t % 3), xb[t % 3][:], xs_d[t * 128:(t + 1) * 128, :], writes=["xb%d" % (t % 3)])

        def rms_and_transpose(src, srckey, ss_col, nwT, nwkey, dstT, dstkeys, xn_t, xnkey, tb):
            ss = sm[:, ss_col:ss_col + 1]
            rs = sm[:, ss_col + 1:ss_col + 2]
            act(lambda e: e.activation(junk, src, AF.Square, accum_out=ss), [srckey], ["ss%d" % ss_col], 1.1)
            pool(lambda e: e.tensor_scalar(rs, ss, 1.0 / D, EPS, ALU.mult, ALU.add), ["ss%d" % ss_col], ["rs%d" % ss_col], 0.2)
            pool(lambda e: e.tensor_tensor(rs, rs, cvec[:, 12:13], ALU.pow), ["rs%d" % ss_col, "cvec"], ["rs%d" % ss_col], 0.5)
            dve(lambda e: e.tensor_scalar(xn_t, src, rs, None, ALU.mult), [srckey, "rs%d" % ss_col], [xnkey], 0.8)
            for k in range(8):
                pe(lambda e, k=k: e.transpose(bf(tb)[:, k, :], xn_t[:, k * 128:(k + 1) * 128], ident[:]), [xnkey, "ident"], [bkey(tb)], 0.1)
            dve(lambda e: e.tensor_tensor(dstT, bf(tb), nwT[:].unsqueeze(2).to_broadcast([128, 8, 128]), ALU.mult), [bkey(tb), nwkey], dstkeys, 1.25)

        pj_i = [0]

        def proj_block(lhsT_t, lhskey, w_t, wkey, c0, ncol, banks=(1, 2)):
            b = banks[pj_i[0] % 2]
            pj_i[0] += 1
            for k in range(8):
                pe(lambda e, k=k, b=b: e.matmul(bank[b][:, 0:ncol], lhsT_t[:, k, :], w_t[:, k, c0:c0 + ncol], start=(k == 0), stop=(k == 7)),
                   [lhskey] + wkey, [bkey(b)], 0.22 if ncol > 256 else (0.13 if ncol > 16 else 0.07))
            return b

        def rotary(b, dst, dstkey, cs, CS):
            ps = bank[b][:].rearrange("p (h t c) -> p h t c", h=4, t=2)
            cosb = cs[:, 64:128].unsqueeze(1).to_broadcast([128, 4, 64])
            sinb = cs[:, 0:64].unsqueeze(1).to_broadcast([128, 4, 64])
            d4 = dst.rearrange("p (h t c) -> p h t c", h=4, t=2)
            dve(lambda e: e.tensor_tensor(rot[0], ps[:, :, 0, :], cosb, ALU.mult), [bkey(b), CS], ["rot0"], 0.37)
            dve(lambda e: e.tensor_tensor(rot[1], ps[:, :, 1, :], sinb, ALU.mult), [bkey(b), CS], ["rot1"], 0.37)
            dve(lambda e: e.tensor_tensor(rot[2], ps[:, :, 1, :], cosb, ALU.mult), [bkey(b), CS], ["rot2"], 0.37)
            dve(lambda e: e.tensor_tensor(rot[3], ps[:, :, 0, :], sinb, ALU.mult), [bkey(b), CS], ["rot3"], 0.37)
            pool(lambda e: e.tensor_tensor(d4[:, :, 0, :], rot[0], rot[1], ALU.subtract), ["rot0", "rot1"], [dstkey + "a"], 0.73)
            pool(lambda e: e.tensor_tensor(d4[:, :, 1, :], rot[2], rot[3], ALU.add), ["rot2", "rot3"], [dstkey + "b"], 0.73)

        def stage_a(t, full):
            S = t % 2
            o = AO[S]
            K_ = lambda nm: "%s_%d" % (nm, S)
            xk = "xb%d" % (t % 3)
            x_t = xb[t % 3]
            E3 = t % 3
            edec = sm[0:64, 8 + 4 * E3:12 + 4 * E3] if E3 < 2 else sm[0:64, 60:64]
            KE = "edec_%d" % E3
            q3 = t % 3 if not full else 0
            xT, XT = [(xT_main, "xT"), (xT_alt1, "xT_alt1"), (xT_alt2, "xT_alt2")][q3]
            cs, CS = [(cs_main, "cs"), (cs_alt1, "cs_alt1"), (cs_alt2, "cs_alt2")][q3]
            use_alt = (not full) and (t % 2 == 0)
            erc, ERC = (erc_alt, "erc_alt") if use_alt else (erc_main, "erc")
            pj_i[0] = 0
            WIN = (WINA + WINB) if full else WINA
            dve(lambda e: e.scalar_tensor_tensor(cs, rotc[:, 0, :], posf[:, t:t + 1], rotc[:, 1, :], ALU.mult, ALU.add), ["rotc0", "rotc1", "posf"], [CS], 0.27)
            dve(lambda e: e.tensor_copy(cki, cs), [CS], ["cki"], 0.27)
            dve(lambda e: e.tensor_copy(ckf, cki), ["cki"], ["ckf"], 0.27)
            dve(lambda e: e.tensor_tensor(cs, cs, ckf, ALU.subtract), [CS, "ckf"], [CS], 0.27)
            dve(lambda e: e.tensor_scalar(ckm, cs, 0.5, None, ALU.is_gt), [CS], ["ckm"], 0.27)
            dve(lambda e: e.tensor_tensor(cs, cs, ckm, ALU.subtract), [CS, "ckm"], [CS], 0.27)
            act(lambda e: e.activation(cs, cs, AF.Sin, scale=2 * math.pi), [CS], [CS], 0.3 if full else 1.6)

            rms_and_transpose(x_t[:], xk, 0, n1wT, "n1wT", xT, [XT], xn, "xn", 0)
            P.cut()

            b = 3
            for k in range(8):
                pe(lambda e, k=k: e.matmul(bank[3][:, 128:144], xT[:, k, :], win[:, k, C_GL:C_GL + 16], start=(k == 0), stop=(k == 7)), [XT] + WIN, [bkey(3)], 0.07)
            dve(lambda e: e.tensor_copy(glb, bank[3][:, 128:144]), [bkey(3)], ["glb"], 0.2)
            pe(lambda e: e.transpose(bank[3][:].bitcast(BF16)[0:16, 0:128], glb, ident[:]), ["glb", "ident"], [bkey(3)], 0.1)
            dve(lambda e: e.tensor_copy(glT[0:16, :], bank[3][:].bitcast(BF16)[0:16, 0:128]), [bkey(3)], ["glT"], 0.2)
            pe(lambda e: e.matmul(bank[3][:, 256:512], glT[:], gw2b[:], start=True, stop=True), ["glT", "gw2b"], [bkey(3)], 0.3)
            act(lambda e: e.activation(e1, bank[3][:, 256:512], AF.Exp, scale=-1.0), [bkey(3)], ["e1"], 1.7)
            act(lambda e: e.activation(nls, e1, AF.Ln, bias=1.0), ["e1"], ["nls"], 0.42)
            if full:
                pe(lambda e: e.matmul(bank[0][:, 0:256], triu[:], nls, start=True, stop=True), ["triu", "nls"], [bkey(0)], 0.3)
            cb = 0 if full else 5
            if not full:
                P.fill(int(os.environ.get("FILL_RC", "12")))
            pe(lambda e: e.matmul(bank[cb][:, 256:512], tril[:], nls, start=True, stop=True), ["tril", "nls"], [bkey(cb)], 0.3)
            for h in range(4):
                pe(lambda e, h=h: e.matmul(bank[3][0:64, 64 + h:65 + h], nls[:, h * 64:(h + 1) * 64], cvec[:, 13:14], start=True, stop=True),
                   ["nls", "cvec"], [bkey(3)], 0.1)
            if full:
                act(lambda e: e.activation(eq, bank[0][:, 0:256], AF.Exp, scale=-1.0 / 16), [bkey(0)], ["eq"], 0.35)
                act(lambda e: e.activation(ek, bank[0][:, 0:256], AF.Exp, scale=1.0 / 16), [bkey(0)], ["ek"], 0.3)
            act(lambda e: e.activation(erc, bank[cb][:, 256:512], AF.Exp, scale=-1.0 / 16), [bkey(cb)], [ERC], 0.3)
            act(lambda e: e.activation(edec, bank[3][0:64, 64:68], AF.Exp, scale=-1.0 / 16), [bkey(3)], [KE], 0.2)

            if full:
                b = proj_block(xT, XT, win, WIN, C_GQ, 512, banks=(0, 0))
                dve(lambda e, b=b: e.scalar_tensor_tensor(o["qg"], bank[b][:, 0:256], 0.125, eq, ALU.mult, ALU.mult), [bkey(b), "eq"], [K_("qg")], 0.42)
                dve(lambda e, b=b: e.tensor_tensor(o["kg"], bank[b][:, 256:512], ek, ALU.mult), [bkey(b), "ek"], [K_("kg")], 0.35)
                dve(lambda e, b=b: e.tensor_tensor(o["kh"], bank[b][:, 256:512], erc, ALU.mult), [bkey(b), ERC], [K_("kh")], 0.35)
            P.cut()
            pj_i[0] = 0
            if not full:
                b = proj_block(xT, XT, win, WIN, C_GK, 256)
                dve(lambda e, b=b: e.tensor_tensor(o["kh"], bank[b][:, 0:256], erc, ALU.mult), [bkey(b), ERC], [K_("kh")], 0.35)
            if full:
                b = proj_block(xT, XT, win, WIN, C_RQ, 512)
                rotary(b, o["qr"], K_("qr"), cs, CS)
            b = proj_block(xT, XT, win, WIN, C_RV, 512)
            for h in range(4):
                act(lambda e, h=h, b=b: e.activation(o["vr"][:, h * 128:(h + 1) * 128], bank[b][:, h * 128:(h + 1) * 128], AF.Identity, scale=cvec[:, h:h + 1]),
                    [bkey(b), "cvec"], [K_("vr")], 0.37)
            b = proj_block(xT, XT, win, WIN, C_RK, 512)
            rotary(b, o["kr"], K_("kr"), cs, CS)
            b = proj_block(xT, XT, win, WIN, C_GV, 512)
            act(lambda e, b=b: e.copy(o["vg"], bank[b][:]), [bkey(b)], [K_("vg")])
            if full:
                b = proj_block(xT, XT, win, WIN, C_RG, 512)
                act(lambda e, b=b: e.activation(o["sgr"], bank[b][:], AF.Silu), [bkey(b)], [K_("sgr")], 1.9)
                b = proj_block(xT, XT, win, WIN, C_GG, 512)
                act(lambda e, b=b: e.activation(o["sgg"], bank[b][:], AF.Silu), [bkey(b)], [K_("sgg")])

        def stage_b(t, full, slot):
            S = t % 2
            o = AO[S]
            K_ = lambda nm: "%s_%d" % (nm, S)
            xk = "xb%d" % (t % 3)
            x_t = xb[t % 3]
            E3 = t % 3
            edec = sm[0:64, 8 + 4 * E3:12 + 4 * E3] if E3 < 2 else sm[0:64, 60:64]
            KE = "edec_%d" % E3
            qr, kr, vr, vg, qg, kg, kh, sgr, sgg = (o[n] for n in ("qr", "kr", "vr", "vg", "qg", "kg", "kh", "sgr", "sgg"))
            KR = [K_("kr") + "a", K_("kr") + "b"]
            QR = [K_("qr") + "a", K_("qr") + "b"]
            if full:
                pool(lambda e: e.tensor_tensor(ng, rnb[:], sgr, ALU.mult), ["rnb", K_("sgr")], ["ng"], 1.26)
                pool(lambda e: e.tensor_tensor(sgr, rnw[:], sgr, ALU.mult), ["rnw", K_("sgr")], [K_("sgr")], 1.26)
                pool(lambda e: e.tensor_tensor(sgg, gnw[:], sgg, ALU.mult), ["gnw", K_("sgg")], [K_("sgg")], 1.26)
            if full:
                for h in range(4):
                    pe(lambda e, h=h: e.transpose(bf(5)[:, h, :], qr[:, h * 128:(h + 1) * 128], ident[:]), QR + ["ident"], [bkey(5)], 0.1)
                    pe(lambda e, h=h: e.transpose(bf(5)[:, 4 + h, :], kr[:, h * 128:(h + 1) * 128], ident[:]), KR + ["ident"], [bkey(5)], 0.1)
                act(lambda e: e.copy(qkT, bf(5)), [bkey(5)], ["qkT"], 0.9)
                for h in range(4):
                    pe(lambda e, h=h: e.matmul(f3(6)[:, h, :], qkT[:, 4 + h, :], qkT[:, h, :], start=True, stop=True), ["qkT"], [bkey(6)], 0.1)
                dve(lambda e: e.tensor_tensor(scm, f3(6), maskr[:].unsqueeze(1).to_broadcast([128, 4, 128]), ALU.mult), [bkey(6), "maskr"], ["scm"], 0.7)
                for h in range(4):
                    pe(lambda e, h=h: e.matmul(f3(7)[:, h, :], scm[:, h, :], vr[:, h * 128:(h + 1) * 128], start=True, stop=False), ["scm", K_("vr")], [bkey(7)], 0.1)
                    pe(lambda e, h=h: e.matmul(f3(7)[:, h, :], qkT[:, h, :], Srb[:, h, :], start=False, stop=True), ["qkT", "Srb"], [bkey(7)], 0.1)
            for h in range(4):
                pe(lambda e, h=h: e.matmul(f3(4)[:, h, :], kr[:, h * 128:(h + 1) * 128], vr[:, h * 128:(h + 1) * 128], start=True, stop=True),
                   KR + [K_("vr")], [bkey(4)], 0.1)
            for h in range(4):
                gc = GAM[h] ** 128
                act(lambda e, h=h, gc=gc: e.activation(dsc[:, h, :], f3(4)[:, h, :], AF.Identity, scale=gc / math.sqrt(128.0)), [bkey(4)], ["dsc"], 0.3)
            for h in range(4):
                gc = GAM[h] ** 128
                dve(lambda e, h=h, gc=gc: e.scalar_tensor_tensor(Sr[:, h, :], Sr[:, h, :], gc, dsc[:, h, :], ALU.mult, ALU.add), ["Sr", "dsc"], ["Sr"], 0.3)
            act(lambda e: e.copy(Srb[:], Sr[:]), ["Sr"], ["Srb"], 0.6)
            P.cut()

            if full:
                for h in range(4):
                    pe(lambda e, h=h: e.transpose(bf(5)[0:64, h, :], qg[:, h * 64:(h + 1) * 64], ident[:]), [K_("qg"), "ident"], [bkey(5)], 0.1)
                    pe(lambda e, h=h: e.transpose(bf(5)[0:64, 4 + h, :], kg[:, h * 64:(h + 1) * 64], ident[:]), [K_("kg"), "ident"], [bkey(5)], 0.1)
                act(lambda e: e.copy(qkgT[0:64], bf(5)[0:64]), [bkey(5)], ["qkgT"], 0.9)
                for h in range(4):
                    pe(lambda e, h=h: e.matmul(f3(6)[:, h, :], qkgT[0:64, 4 + h, :], qkgT[0:64, h, :], start=True, stop=True), ["qkgT"], [bkey(6)], 0.1)
                dve(lambda e: e.tensor_tensor(scm, f3(6), maskg[:].unsqueeze(1).to_broadcast([128, 4, 128]), ALU.mult), [bkey(6), "maskg"], ["scm"], 0.7)
                for h in range(4):
                    pe(lambda e, h=h: e.matmul(f3(4)[:, h, :], scm[:, h, :], vg[:, h * 128:(h + 1) * 128], start=True, stop=False), ["scm", K_("vg")], [bkey(4)], 0.1)
                    pe(lambda e, h=h: e.matmul(f3(4)[:, h, :], qkgT[0:64, h, :], Sgb[:, h, :], start=False, stop=True), ["qkgT", "Sgb"], [bkey(4)], 0.1)
            for h in range(4):
                pe(lambda e, h=h: e.matmul(f3(6)[0:64, h, :], kh[:, h * 64:(h + 1) * 64], vg[:, h * 128:(h + 1) * 128], start=True, stop=True), [K_("kh"), K_("vg")], [bkey(6)], 0.1)
            for h in range(4):
                dve(lambda e, h=h: e.scalar_tensor_tensor(Sg[:, h, :], Sg[:, h, :], edec[:, h:h + 1], f3(6)[0:64, h, :], ALU.mult, ALU.add),
                    ["Sg", KE, bkey(6)], ["Sg"], 0.3)
            act(lambda e: e.copy(Sgb[:], Sg[:]), ["Sg"], ["Sgb"], 0.45)
            P.cut()
            if not full:
                return

            st6 = sm[:, 16:40].rearrange("p (h c) -> p h c", c=6)
            mv = sm[:, 40:48].rearrange("p (h c) -> p h c", c=2)
            for h in range(4):
                dve(lambda e, h=h: e.bn_stats(st6[:, h, :], f3(7)[:, h, :]), [bkey(7)], ["st6"], 0.3)
            for h in range(4):
                dve(lambda e, h=h: e.bn_aggr(mv[:, h, :], st6[:, h, :]), ["st6"], ["mv"], 0.18)
            rsr = sm[:, 48:52]
            dve(lambda e: e.tensor_tensor(rsr, mv[:, :, 1], cvec[:, 8:12], ALU.mult), ["mv", "cvec"], ["rsr"], 0.17)
            dve(lambda e: e.tensor_scalar(rsr, rsr, EPS, None, ALU.add), ["rsr"], ["rsr"], 0.17)
            pool(lambda e: e.tensor_tensor(rsr, rsr, cvec[:, 12:13].to_broadcast([128, 4]), ALU.pow), ["rsr", "cvec"], ["rsr"], 1.0)
            dve(lambda e: e.tensor_tensor(rsr, rsr, cvec[:, 4:8], ALU.mult), ["rsr", "cvec"], ["rsr"], 0.17)
            for h in range(4):
                dve(lambda e, h=h: e.tensor_scalar(nr[:, h * 128:(h + 1) * 128], f3(7)[:, h, :], mv[:, h, 0:1], rsr[:, h:h + 1], ALU.subtract, ALU.mult),
                    [bkey(7), "mv", "rsr"], ["nr"], 0.41)
            dve(lambda e: e.tensor_tensor(nr, nr, sgr, ALU.mult), ["nr", K_("sgr")], ["nr"], 0.69)
            dve(lambda e: e.tensor_tensor(mixed[:, 0:512], nr, ng, ALU.add), ["nr", "ng"], ["mixeda"], 0.69)
            P.cut()
            ssg = sm[:, 52:56]
            rsg = sm[:, 56:60]
            for h in range(4):
                act(lambda e, h=h: e.activation(junk[:, h * 128:(h + 1) * 128], f3(4)[:, h, :], AF.Square, accum_out=ssg[:, h:h + 1]), [bkey(4)], ["ssg"], 0.37)
            pool(lambda e: e.tensor_scalar(rsg, ssg, 1.0 / 128, EPS, ALU.mult, ALU.add), ["ssg"], ["rsg"], 0.2)
            pool(lambda e: e.tensor_tensor(rsg, rsg, cvec[:, 12:13].to_broadcast([128, 4]), ALU.pow), ["rsg", "cvec"], ["rsg"], 1.0)
            for h in range(4):
                dve(lambda e, h=h: e.scalar_tensor_tensor(mixed[:, 512 + h * 128:512 + (h + 1) * 128], f3(4)[:, h, :], rsg[:, h:h + 1], sgg[:, h * 128:(h + 1) * 128], ALU.mult, ALU.mult),
                    [bkey(4), "rsg", K_("sgg")], ["mixedb"], 0.35)
            P.cut()
            for k in range(8):
                pe(lambda e, k=k: e.transpose(bf(5)[:, k, :], mixed[:, k * 128:(k + 1) * 128], ident[:]), ["mixeda", "mixedb", "ident"], [bkey(5)], 0.1)
            act(lambda e: e.copy(mT, bf(5)), [bkey(5)], ["mT"], 0.9)
            hk = "hb%d" % slot
            pj_b = [0]
            for n in range(2):
                b = (6, 7)[n]
                for k in range(8):
                    pe(lambda e, k=k, b=b, n=n: e.matmul(bank[b][:], mT[:, k, :], wout[:, k, n * 512:(n + 1) * 512], start=(k == 0), stop=(k == 7)),
                       ["mT"] + WOUT, [bkey(b)])
                dve(lambda e, b=b, n=n: e.tensor_tensor(hb[slot][:, n * 512:(n + 1) * 512], x_t[:, n * 512:(n + 1) * 512], bank[b][:], ALU.add),
                    [bkey(b), xk], [hk], 0.69)
            rms_and_transpose(hb[slot][:], hk, 2, n2wT, "n2wT", hnT[:, :, slot * 128:(slot + 1) * 128], ["hnT%d" % slot], hn, "hn", 5)

        wu_i = [0]
        wd_i = [0]
        CAR = ["carry%d" % c for c in range(2 * NG)]

        pref = []

        def up_load(blk):
            s = wu_i[0] % NWU
            wu_i[0] += 1
            P.dma("sp", "wul%d_0" % s, wu[s][:, 0], wup_sv[:, :, blk * 256:(blk + 1) * 256], reads=WUPS, writes=["wu%d_0" % s])
            P.dma("sp", "wul%d_1" % s, wu[s][:, 1], wup_sv[:, :, DFF + blk * 256:DFF + (blk + 1) * 256], reads=WUPS, writes=["wu%d_1" % s])
            return s

        def prefetch_up():
            del pref[:]
            for blk in range(NWU):
                pref.append(up_load(blk))

        def ffn(ntl, out_tile0):
            halo = out_tile0 < 0
            ntok = ntl * 128 if not halo else 2
            tok0 = 0 if not halo else 126
            hkeys = ["hnT%d" % j for j in range(ntl)]
            if not halo:
                pool(lambda e: e.tensor_tensor(corr[:, :, 0], carry[:, :, 0], cwt[:, 0, :], ALU.mult), CAR + ["cwt0"], ["corr"])
                pool(lambda e: e.tensor_tensor(ctmp[:], carry[:, :, 1], cwt[:, 1, :], ALU.mult), CAR + ["cwt1"], ["ctmp"])
                pool(lambda e: e.tensor_tensor(corr[:, :, 0], corr[:, :, 0], ctmp[:], ALU.add), ["corr", "ctmp"], ["corr"])
                pool(lambda e: e.tensor_tensor(corr[:, :, 1], carry[:, :, 1], cwt[:, 0, :], ALU.mult), CAR + ["cwt0"], ["corr"])
            pair_i = 0
            for blk in range(NG // 2):
                if blk < len(pref):
                    s = pref[blk]
                else:
                    s = up_load(blk)
                for gi in range(2):
                    g = blk * 2 + gi
                    pp = pair_i % 3
                    pair_i += 1
                    bg, bv = 2 * pp, 2 * pp + 1
                    ag, av = acc[2 * pp], acc[2 * pp + 1]
                    kag, kav = "acc%d" % (2 * pp), "acc%d" % (2 * pp + 1)
                    for half, bb_ in ((0, bg), (1, bv)):
                        for k in range(8):
                            pe(lambda e, k=k, half=half, bb_=bb_, s=s, gi=gi: e.matmul(bank[bb_][:, 0:ntok], wu[s][:, half, k, gi * 128:(gi + 1) * 128], hnT[:, k, tok0:tok0 + ntok],
                                                                                     start=(k == 0), stop=(k == 7)), hkeys + ["wu%d_%d" % (s, half)], [bkey(bb_)], 0.22 if not halo else 0.07)
                    for (a, ka, bb_, ch) in ((ag, kag, bg, g), (av, kav, bv, NG + g)):
                        if not halo:
                            act(lambda e, a=a, bb_=bb_, ch=ch: e.activation(a[:, 0:ntok], bank[bb_][:, 0:ntok], AF.Identity, bias=cwt[:, 3, ch:ch + 1], scale=cwt[:, 2, ch:ch + 1]),
                                [bkey(bb_), "cwt2", "cwt3"], [ka], 0.62)
                            dve(lambda e, a=a, bb_=bb_, ch=ch: e.scalar_tensor_tensor(a[:, 1:ntok], bank[bb_][:, 0:ntok - 1], cwt[:, 1, ch:ch + 1], a[:, 1:ntok], ALU.mult, ALU.add),
                                [bkey(bb_), ka, "cwt1"], [ka], 0.7)
                            dve(lambda e, a=a, bb_=bb_, ch=ch: e.scalar_tensor_tensor(a[:, 2:ntok], bank[bb_][:, 0:ntok - 2], cwt[:, 0, ch:ch + 1], a[:, 2:ntok], ALU.mult, ALU.add),
                                [bkey(bb_), ka, "cwt0"], [ka], 0.7)
                            pool(lambda e, a=a, ch=ch: e.tensor_tensor(a[:, 0:2], a[:, 0:2], corr[:, ch, :], ALU.add), [ka, "corr"], [ka], 0.2)
                        act(lambda e, bb_=bb_, ch=ch: e.copy(carry[:, ch, :], bank[bb_][:, ntok - 2:ntok]), [bkey(bb_), "corr"], ["carry%d" % ch], 0.25)
                    if halo:
                        continue
                    act(lambda e, a=ag: e.activation(a[:, 0:ntok], a[:, 0:ntok], AF.Silu), [kag], [kag], 0.62)
                    pool(lambda e, g=g, a=ag, a2=av: e.tensor_tensor(actT[g][:, 0:ntok], a[:, 0:ntok], a2[:, 0:ntok], ALU.mult), [kag, kav], ["actT%d" % g], 1.26)
            del pref[:]
            if halo:
                prefetch_up()
                return
            if os.environ.get("KDEBUG") and out_tile0 == 12:
                print("ffn up done", {k: round(v, 1) for k, v in P.t_eng.items()})
            dbank = [6, 7, 0, 1]
            for n in range(2):
                for k0 in range(0, NG, 8):
                    nk = min(8, NG - k0)
                    s = wu_i[0] % NWU
                    wu_i[0] += 1
                    wv = wu[s][:].rearrange("p a k c -> p (a k c)").rearrange("p (k c) -> p k c", c=512)
                    P.dma("sp", "wul%d_0" % s, wv[:, 0:nk, :], wdn_sv[:, k0:k0 + nk, n * 512:(n + 1) * 512], reads=WDNS, writes=["wu%d_0" % s, "wu%d_1" % s])
                    for kk in range(nk):
                        k = k0 + kk
                        for j in range(ntl):
                            pe(lambda e, k=k, kk=kk, j=j, wv=wv: e.matmul(bank[dbank[j]][:], actT[k][:, j * 128:(j + 1) * 128], wv[:, kk, :],
                                                                         start=(k == 0), stop=(k == NG - 1)), ["actT%d" % k, "wu%d_0" % s, "wu%d_1" % s], [bkey(dbank[j])])
                for j in range(ntl):
                    dve(lambda e, j=j, n=n: e.tensor_tensor(hb[j][:, n * 512:(n + 1) * 512], hb[j][:, n * 512:(n + 1) * 512], bank[dbank[j]][:], ALU.add),
                        [bkey(dbank[j]), "hb%d" % j], ["hb%d" % j], 0.69)
            if os.environ.get("KDEBUG") and out_tile0 == 12:
                print("ffn down done", {k: round(v, 1) for k, v in P.t_eng.items()})
            for j in range(ntl):
                hk = "hb%d" % j
                ss = sm[:, 4:5]
                rs = sm[:, 5:6]
                act(lambda e, j=j: e.activation(junk, hb[j][:], AF.Square, accum_out=ss), [hk], ["ss4"], 1.1)
                pool(lambda e: e.tensor_scalar(rs, ss, 1.0 / D, EPS, ALU.mult, ALU.add), ["ss4"], ["rs4"], 0.2)
                pool(lambda e: e.tensor_tensor(rs, rs, cvec[:, 12:13], ALU.pow), ["rs4", "cvec"], ["rs4"], 0.5)
                dve(lambda e, j=j: e.scalar_tensor_tensor(hb[j][:], hb[j][:], rs, fnw[:], ALU.mult, ALU.mult), [hk, "rs4", "fnw"], [hk], 1.3)
                r0 = (out_tile0 + j) * 128
                P.dma("sp", "yst%d" % j, y_d[r0:r0 + 128, :], hb[j][:], reads=[hk])
            if out_tile0 + ntl < NMAIN:
                prefetch_up()

        tiles = [(t, False, 0, None) for t in range(NPRE)]
        tiles.append((NPRE, True, 0, (1, -1)))
        for s in range(NSUP):
            for j in range(NT):
                tiles.append((NPRE + 1 + s * NT + j, True, j, (NT, s * NT) if j == NT - 1 else None))

        def rec_a(i):
            t, full, slot, f = tiles[i]
            P.record()
            if t + 1 < NV:
                load_x(t + 1)
            stage_a(t, full)
            return P.stop()

        def rec_b(i):
            t, full, slot, f = tiles[i]
            P.record()
            stage_b(t, full, slot)
            return P.stop()

        def stg_a(i):
            t, full, slot, f = tiles[i]
            P.record()
            if t + 1 < NV:
                load_x(t + 1)
            stage_a(t, full)
            c = P.stop_chunks()
            if full or not SPLIT:
                return [], [], [[c[0]], [c[1], c[2]]]
            return [[c[0]]], [[c[1]]], [[c[2]]]

        def stg_b(i):
            t, full, slot, f = tiles[i]
            P.record()
            stage_b(t, full, slot)
            c = P.stop_chunks()
            if not full:
                return [[c[0]], [c[1]]]
            return [[c[0]], [c[1], c[2]], [c[3]], [c[4]]]

        PIPE = int(os.environ.get("PIPE", "3"))
        SPLIT = bool(int(os.environ.get("SPLIT", "1")))
        FILL = bool(int(os.environ.get("FILL", "0")))
        assert NPRE % 2 == 1
        tails = {}
        BULK_PER = (len(bulk) + max(1, NPRE // 2) - 1) // max(1, NPRE // 2)
        stA = {}
        if PIPE == 3:
            stA[0] = stg_a(0)
            stA[1] = stg_a(1)
            if stA[0][0]:
                P.play_stages([stA[0][0]])
            P.play_stages([st_ for st_ in (stA[0][1], stA[1][0]) if st_])
        else:
            P.play(rec_a(0))
        for i in range(len(tiles) + (1 if PIPE == 3 else 0)):
            if PIPE != 3:
                ra = rec_a(i + 1) if i + 1 < len(tiles) else []
                rb = rec_b(i)
            if PIPE == 3:
                P.filler = filler if (FILL and i < NPRE - 1) else None
                issue_bulk(BULK_PER if i < NPRE - 2 else len(bulk))
                if i == NPRE - 2:
                    prefetch_up()
                if i + 2 < len(tiles):
                    stA[i + 2] = stg_a(i + 2)
                h1 = stA[i + 2][0] if i + 2 in stA else []
                h2 = stA[i + 1][1] if i + 1 in stA else []
                tl = stA[i][2] if i in stA else []
                P.play_stages([st_ for st_ in (stg_b(i - 1) if i >= 1 else [], tl, h2, h1) if st_])
                stA.pop(i, None)
                if os.environ.get("KDEBUG") and (i in (8, 9, 10, 11) or 44 <= i <= 46):
                    print("after step", i, {k: round(v, 1) for k, v in P.t_eng.items()})
                if i >= 1 and tiles[i - 1][3] is not None:
                    ffn(*tiles[i - 1][3])
                continue
            if False:
                if os.environ.get("KDEBUG") and (44 <= i <= 48 or i in (5, 6, 7)):
                    print("after tile", i, {k: round(v, 1) for k, v in P.t_eng.items()})
                if tiles[i][3] is not None:
                    ffn(*tiles[i][3])
                continue
            if PIPE == 2:
                P.play_merged(rb, ra)
            elif PIPE == 1:
                P.play(interleave(ra, rb))
            else:
                P.play(rb + ra)
            if os.environ.get("KDEBUG") and 44 <= i <= 53:
                print("after tile", i, {k: round(v, 1) for k, v in P.t_eng.items()})
            if tiles[i][3] is not None:
                ffn(*tiles[i][3])
                if os.environ.get("KDEBUG") and 44 <= i <= 53:
                    print("after ffn", i, {k: round(v, 1) for k, v in P.t_eng.items()})
        for j in range(NT):
            ent = P.dma_sems["yst%d" % j]
            P.streams["sp"].append(("wait", ent[0], ent[1]))
        if os.environ.get("KDEBUG"):
            print("fillers:", P.n_fill)
            print("MODEL est total us:", round(max(P.t_eng.values()), 1), {k: round(v, 1) for k, v in P.t_eng.items()}, "n_ops", P.cnt)
        P.emit()
    return nc


def _consts():
    l = np.arange(128, dtype=np.float64)
    c = {}
    c["c_ident"] = np.eye(128, dtype=np.float32)
    le = (l[:, None] <= l[None, :])
    c["c_maskr"] = (le / math.sqrt(128.0)).astype(np.float32)
    c["c_maskg"] = le.astype(np.float32)
    c["c_triu"] = le.astype(np.float32)
    c["c_tril"] = (l[:, None] > l[None, :]).astype(np.float32)
    invf = 10000.0 ** (-np.arange(64, dtype=np.float64) / 64.0)
    rot = np.zeros((2, 128), np.float64)
    rot[0, :64] = invf / (2 * math.pi)
    rot[0, 64:] = invf / (2 * math.pi)
    rot[1, 64:] = 0.25
    c["c_rot"] = rot.astype(np.float32)
    cv = np.zeros((128, 16), np.float64)
    for h in range(4):
        cv[:, h] = GAM[h] ** (-(l + 1))
        cv[:, 4 + h] = GAM[h] ** (l + 1)
        cv[:, 8 + h] = GAM[h] ** (2 * (l + 1))
    cv[:, 12] = -0.5
    cv[:, 13] = 1.0
    c["c_vec"] = cv.astype(np.float32)
    return c


_NC_CACHE = {}


def _run(inputs, NPRE, NSUP, seq):
    x = np.asarray(inputs["x"], dtype=np.float32)
    pos = np.asarray(inputs["positions"], dtype=np.int32)
    B = x.shape[0]
    M = NT * NSUP * 128
    NV = NPRE + 1 + NT * NSUP
    assert seq == 2 * M and (NPRE + 1) * 128 == M
    key = (NPRE, NSUP)
    if key not in _NC_CACHE:
        _NC_CACHE[key] = build(NPRE, NSUP)
    nc = _NC_CACHE[key]
    consts = _consts()
    shared = {
        "norm1_w": np.ascontiguousarray(inputs["norm1_w"][0], np.float32),
        "norm2_w": np.ascontiguousarray(inputs["norm2_w"][0], np.float32),
        "final_norm_w": np.ascontiguousarray(inputs["final_norm_w"], np.float32).reshape(1, D),
        "w_in": np.ascontiguousarray(inputs["w_in"][0], np.float32),
        "w_out": np.ascontiguousarray(inputs["w_out"][0], np.float32),
        "ffn_w_up": np.ascontiguousarray(inputs["ffn_w_up"][0], np.float32),
        "ffn_w_down": np.ascontiguousarray(inputs["ffn_w_down"][0], np.float32),
        "ret_norm_w": np.ascontiguousarray(inputs["ret_norm_w"][0], np.float32).reshape(1, 512),
        "ret_norm_b": np.ascontiguousarray(inputs["ret_norm_b"][0], np.float32).reshape(1, 512),
        "gla_norm_w": np.ascontiguousarray(inputs["gla_norm_w"][0], np.float32).reshape(1, 512),
        "gla_gate_w2": np.ascontiguousarray(inputs["gla_gate_w2"][0], np.float32),
        "gla_gate_b": np.ascontiguousarray(inputs["gla_gate_b"][0], np.float32).reshape(1, 256),
        "ffn_conv_w": np.ascontiguousarray(inputs["ffn_conv_w"][0], np.float32),
        "ffn_conv_b": np.ascontiguousarray(inputs["ffn_conv_b"][0], np.float32),
    }
    shared.update(consts)
    in_maps = []
    for core in range(2 * B):
        b, half = core // 2, core % 2
        if half == 0:
            xs = np.concatenate([np.zeros((M, D), np.float32), x[b, :M]], axis=0)
            ps = np.concatenate([np.zeros((M,), np.int32), pos[b, :M]], axis=0)
        else:
            xs = x[b]
            ps = pos[b]
        m = dict(shared)
        m["xs"] = np.ascontiguousarray(xs)
        m["pos"] = np.ascontiguousarray(ps.reshape(NV, 128))
        in_maps.append(m)
    res = run_bass_kernel_spmd(nc, in_maps, core_ids=list(range(2 * B)))
    out = np.empty((B, seq, D), np.float32)
    for core in range(2 * B):
        b, half = core // 2, core % 2
        out[b, half * M:(half + 1) * M] = res.results[core]["y"]
    return out


def kernel(**inputs):
    return _run(inputs, NPRE=31, NSUP=8, seq=8192)
```

```python
import math
import os
from contextlib import ExitStack

import numpy as np
import concourse.bass as bass
import concourse.mybir as mybir
from concourse.bass_utils import run_bass_kernel_spmd

F32 = mybir.dt.float32
BF16 = mybir.dt.bfloat16
I32 = mybir.dt.int32
ALU = mybir.AluOpType
AF = mybir.ActivationFunctionType

D = 1024
DIN = 3600
DFF = 2816
NG = 22
EPS = 1e-6
NT = 4
GAM = [1.0 - 2.0 ** (-5.0 - h) for h in range(4)]
C_RQ, C_RK, C_RV, C_RG, C_GQ, C_GK, C_GV, C_GG, C_GL = 0, 512, 1024, 1536, 2048, 2304, 2560, 3072, 3584


class Prog:
    ENG = ("pe", "act", "dve", "pool", "sp")

    def __init__(self, nc, stack):
        self.nc = nc
        self.stack = stack
        self.streams = {e: [] for e in self.ENG}
        self.sem = {e: stack.enter_context(nc.semaphore("s_" + e)) for e in self.ENG if e != "sp"}
        self.cnt = {e: 0 for e in self.ENG}
        self.waited = {e: {} for e in self.ENG}
        self.last_w = {}
        self.readers = {}
        self.dma_sems = {}
        self.alias = {}
        self.regions = []
        self.strict_same = bool(int(os.environ.get("STRICT_SAME", "1")))
        self.rec = None
        self.filler = None
        self.in_fill = False
        self.n_fill = 0
        self.fill_min = float(os.environ.get("FILL_MIN", "0.7"))
        self.fill_frac = float(os.environ.get("FILL_FRAC", "0.7"))
        self.fill_max = int(os.environ.get("FILL_MAX", "16"))
        self.t_w = {}
        self.t_r = {}
        self.t_eng = {e: 0.0 for e in self.ENG}

    def region(self, key, start, end):
        for (k2, s2, e2) in self.regions:
            if start < e2 and s2 < end:
                self.alias.setdefault(key, []).append(k2)
                self.alias.setdefault(k2, []).append(key)
        self.regions.append((key, start, end))

    def new_dma_sem(self, name, total=None):
        s = self.stack.enter_context(self.nc.semaphore(name))
        self.dma_sems[name] = [s, 0, None if total is None else 16 * total]
        return name

    def _deps(self, eng, reads, writes):
        deps = []
        for r in reads:
            for k in [r] + self.alias.get(r, []):
                t = self.last_w.get(k)
                if t is not None:
                    deps.append((t, True))
        for w in writes:
            for k in [w] + self.alias.get(w, []):
                t = self.last_w.get(k)
                if t is not None:
                    deps.append((t, False))
                for t in self.readers.get(k, ()):
                    deps.append((t, False))
        need = {}
        for (semname, val, semobj, teng), is_raw in deps:
            if teng == eng:
                if eng in ("pe", "sp"):
                    continue
                if not is_raw and not self.strict_same:
                    continue
            if self.waited[eng].get(semname, 0) >= val:
                continue
            if need.get(semname, (0, None))[0] < val:
                need[semname] = (val, semobj)
        for semname, (val, semobj) in need.items():
            self.waited[eng][semname] = val
            self.streams[eng].append(("wait", semobj, val))

    def _commit(self, token, reads, writes):
        for w in writes:
            self.last_w[w] = token
            self.readers[w] = []
        for r in reads:
            if r not in writes:
                self.readers.setdefault(r, []).append(token)

    def record(self):
        self.rec = []
        self.cuts = []

    def cut(self):
        if self.rec is not None:
            self.cuts.append(len(self.rec))

    def stop_chunks(self):
        r, cuts = self.rec, self.cuts
        self.rec = None
        out, prev = [], 0
        for c in cuts + [len(r)]:
            out.append(r[prev:c])
            prev = c
        return out

    def stop(self):
        r, self.rec = self.rec, None
        return r

    DEFC = {"pe": 0.27, "act": 0.6, "dve": 0.5, "pool": 1.2, "sp": 0.1}

    def _est_start(self, eng, reads, writes):
        t = self.t_eng[eng]

        def lat(e2):
            if e2 != eng:
                return 0.2
            return 0.0 if eng in ("pe", "sp") else 0.08
        for r in reads:
            for k in [r] + self.alias.get(r, []):
                v = self.t_w.get(k)
                if v is not None:
                    t = max(t, v[0] + lat(v[1]))
        for w in writes:
            for k in [w] + self.alias.get(w, []):
                v = self.t_w.get(k)
                if v is not None:
                    t = max(t, v[0] + lat(v[1]))
                v = self.t_r.get(k)
                if v is not None:
                    t = max(t, v[0] + lat(v[1]))
        return t

    def _est_commit(self, eng, reads, writes, cost, is_dma=False):
        t0 = self._est_start(eng, reads, writes)
        if os.environ.get("KTRACE") and float(os.environ["KTRACE"].split(",")[0]) <= t0 <= float(os.environ["KTRACE"].split(",")[1]):
            if t0 - self.t_eng[eng] > float(os.environ.get("KSTALL", "0")):
                blk = None
                for k in list(reads) + list(writes):
                    for kk in [k] + self.alias.get(k, []):
                        for dct in (self.t_w, self.t_r):
                            v = dct.get(kk)
                            if v is not None and (blk is None or v[0] > blk[0]):
                                blk = (v[0], v[1], kk)
                print("  %8.2f %-4s %5.2f stall=%.2f blocked_by=%s w=%s" % (t0, eng, cost, t0 - self.t_eng[eng], blk, list(writes)[:2]))
        if is_dma:
            self.t_eng[eng] = t0 + 0.1
            t1 = t0 + cost
        else:
            t1 = t0 + cost
            self.t_eng[eng] = t1
        we = "dma" if is_dma else eng
        for w in writes:
            self.t_w[w] = (t1, we)
            self.t_r.pop(w, None)
        for r in reads:
            if r not in writes:
                if r not in self.t_r or self.t_r[r][0] < t1:
                    self.t_r[r] = (t1, we)

    def _rec_info(self, r):
        if r[0] in ("op", "fill"):
            return r[1], r[3], r[4]
        return r[1], r[5], r[6]

    def play(self, recs):
        for r in recs:
            if r[0] == "op":
                self.op(*r[1:])
            elif r[0] == "fill":
                if self.filler is not None:
                    self.in_fill = True
                    for _ in range(r[6]):
                        self.filler()
                    self.in_fill = False
                    self.n_fill += r[6]
            else:
                self.dma(*r[1:])

    def fill(self, n):
        if self.rec is not None:
            self.rec.append(("fill", "pe", None, [], [], 0.0, n))

    def play_stages(self, stages):
        state = [[0, [0] * len(st[0])] if st else None for st in stages]
        while True:
            best = None
            for si, st in enumerate(stages):
                if state[si] is None:
                    continue
                ph, ptrs = state[si]
                while ph < len(st) and all(ptrs[ci] >= len(st[ph][ci]) for ci in range(len(st[ph]))):
                    ph += 1
                    if ph < len(st):
                        ptrs = [0] * len(st[ph])
                if ph >= len(st):
                    state[si] = None
                    continue
                state[si] = [ph, ptrs]
                for ci, ch in enumerate(st[ph]):
                    if ptrs[ci] < len(ch):
                        e, r, w = self._rec_info(ch[ptrs[ci]])
                        t = self._est_start(e, r, w)
                        if best is None or t < best[0]:
                            best = (t, si, ci)
            if best is None:
                break
            _, si, ci = best
            ph, ptrs = state[si]
            self.play([stages[si][ph][ci][ptrs[ci]]])
            ptrs[ci] += 1

    def play_merged(self, a, b):
        ia = ib = 0
        while ia < len(a) or ib < len(b):
            if ib >= len(b):
                pick_a = True
            elif ia >= len(a):
                pick_a = False
            else:
                ea, ra, wa = self._rec_info(a[ia])
                eb, rb, wb = self._rec_info(b[ib])
                pick_a = self._est_start(ea, ra, wa) <= self._est_start(eb, rb, wb)
            if pick_a:
                self.play([a[ia]]); ia += 1
            else:
                self.play([b[ib]]); ib += 1

    def op(self, eng, fn, reads=(), writes=(), c=None):
        if self.rec is not None:
            self.rec.append(("op", eng, fn, list(reads), list(writes), c))
            return
        if eng == "pe" and self.filler is not None and not self.in_fill:
            stall = self._est_start(eng, reads, writes) - self.t_eng["pe"]
            if stall >= self.fill_min:
                n = min(self.fill_max, int(self.fill_frac * stall / 0.22))
                self.in_fill = True
                for _ in range(n):
                    self.filler()
                self.in_fill = False
                self.n_fill += n
        self._est_commit(eng, reads, writes, c if c is not None else self.DEFC[eng])
        self._deps(eng, reads, writes)
        self.cnt[eng] += 1
        token = ("s_" + eng, self.cnt[eng], self.sem[eng], eng)
        self.streams[eng].append(("op", fn, self.sem[eng], 1))
        self._commit(token, reads, writes)

    def dma(self, eng, semname, out, in_, reads=(), writes=(), slow=False):
        if self.rec is not None:
            self.rec.append(("dma", eng, semname, out, in_, list(reads), list(writes), slow))
            return
        self._est_commit(eng, reads, writes, 2.5, is_dma=True)
        self._deps(eng, reads, writes)
        ent = self.dma_sems[semname]
        ent[1] += 16
        token = (semname, ent[2] if ent[2] is not None else ent[1], ent[0], "dma")
        self.streams[eng].append(("op", lambda e, o=out, i=in_, s=slow: e.dma_start(o, i, allow_slow_non_contiguous=s), ent[0], 16))
        self._commit(token, reads, writes)

    def emit(self):
        nc = self.nc
        for name, ent in self.dma_sems.items():
            assert ent[2] is None or ent[2] == ent[1], (name, ent[1], ent[2])
        engmap = {"pe": "tensor", "act": "scalar", "dve": "vector", "pool": "gpsimd", "sp": "sync"}
        with nc.Block() as block:
            for e in self.ENG:
                items = self.streams[e]

                def body(engine, items=items):
                    for it in items:
                        if it[0] == "wait":
                            engine.wait_ge(it[1], it[2])
                        else:
                            it[1](engine).then_inc(it[2], it[3])
                getattr(block, engmap[e])(body)


def interleave(a, b):
    out = []
    na, nb = len(a), len(b)
    ia = ib = 0
    while ia < na or ib < nb:
        if ib >= nb or (ia < na and ia * nb <= ib * na):
            out.append(a[ia]); ia += 1
        else:
            out.append(b[ib]); ib += 1
    return out


def build(NPRE, NSUP):
    NV = NPRE + 1 + NT * NSUP
    NMAIN = NT * NSUP
    nc = bass.Bass("TRN2", target_bir_lowering=False)
    din = lambda name, shape, dt=F32: nc.dram_tensor(name, shape, dt, kind="ExternalInput").ap()
    xs_d = din("xs", [NV * 128, D])
    pos_d = din("pos", [NV, 128], I32)
    n1w_d = din("norm1_w", [D]); n2w_d = din("norm2_w", [D]); fnw_d = din("final_norm_w", [1, D])
    win_d = din("w_in", [D, DIN]); wout_d = din("w_out", [D, D])
    wup_d = din("ffn_w_up", [D, 2 * DFF]); wdn_d = din("ffn_w_down", [DFF, D])
    rnw_d = din("ret_norm_w", [1, 512]); rnb_d = din("ret_norm_b", [1, 512]); gnw_d = din("gla_norm_w", [1, 512])
    gw2_d = din("gla_gate_w2", [16, 256]); gb_d = din("gla_gate_b", [1, 256])
    cw_d = din("ffn_conv_w", [3, 2 * DFF]); cb_d = din("ffn_conv_b", [2 * DFF])
    ident_d = din("c_ident", [128, 128]); maskr_d = din("c_maskr", [128, 128]); maskg_d = din("c_maskg", [128, 128])
    triu_d = din("c_triu", [128, 128]); tril_d = din("c_tril", [128, 128])
    rot_d = din("c_rot", [2, 128]); cvec_d = din("c_vec", [128, 16])
    y_d = nc.dram_tensor("y", [NMAIN * 128, D], F32, kind="ExternalOutput").ap()
    wup_s = nc.dram_tensor("wup_s", [D, 2 * DFF], BF16, kind="Internal").ap()
    wdn_s = nc.dram_tensor("wdn_s", [DFF, D], BF16, kind="Internal").ap()

    with ExitStack() as st:
        P = Prog(nc, st)
        sb_bytes = [0]

        def T(name, shape, dt):
            n = 1
            for d_ in shape[1:]:
                n *= d_
            sb_bytes[0] += n * (2 if dt == BF16 else 4)
            return st.enter_context(nc.sbuf_tensor(name, shape, dt))
        bank = [st.enter_context(nc.psum_tensor("bank%d" % i, [128, 512], F32)) for i in range(8)]
        bkey = lambda i: "bank%d" % i
        bf = lambda i: bank[i][:].bitcast(BF16).rearrange("p (k c) -> p k c", c=128)
        f3 = lambda i: bank[i][:].rearrange("p (h c) -> p h c", c=128)

        win = T("win", [128, 8, DIN], BF16)
        wout = T("wout", [128, 8, D], BF16)
        NWU, NWD = 2, 2
        wu = [T("wu%d" % i, [128, 2, 8, 256], BF16) for i in range(NWU)]
        fnw = T("fnw", [128, D], F32)
        rnw = T("rnw", [128, 512], F32); rnb = T("rnb", [128, 512], F32); gnw = T("gnw", [128, 512], F32)
        n1wT = T("n1wT", [128, 8], F32); n2wT = T("n2wT", [128, 8], F32)
        identf = T("identf", [128, 128], F32); ident = T("ident", [128, 128], BF16)
        maskr = T("maskr", [128, 128], F32); maskg = T("maskg", [128, 128], F32)
        triu = T("triu", [128, 128], F32); tril = T("tril", [128, 128], F32)
        rotc = T("rotc", [128, 2, 128], F32)
        cvec = T("cvec", [128, 16], F32)
        gw2f = T("gw2f", [17, 256], F32); gw2b = T("gw2b", [17, 256], BF16)
        cwt = T("cwt", [128, 4, 2 * NG], F32)
        posi = T("posi", [NV, 128], I32); posr = T("posr", [NV, 128], F32); posf = T("posf", [128, NV], F32)
        xb = [T("xb%d" % i, [128, D], F32) for i in range(3)]
        hb = [T("hb%d" % i, [128, D], F32) for i in range(NT)]
        hnT = T("hnT", [128, 8, NT * 128], BF16)
        Sr = T("Sr", [128, 4, 128], F32); Srb = T("Srb", [128, 4, 128], BF16)
        Sg = T("Sg", [64, 4, 128], F32); Sgb = T("Sgb", [64, 4, 128], BF16)
        carry = T("carry", [128, 2 * NG, 2], F32)
        sm = T("sm", [128, 64], F32)
        glT = T("glT", [17, 128], BF16)

        KEEP_BYTES = 2 * (4 * 1024 + 2 * 2048 + 3 * 512)
        UNI_BYTES = KEEP_BYTES + 35328
        uni = T("uni", [128, UNI_BYTES // 4], F32)
        off = {"keep": 0, "mix": KEEP_BYTES, "ffn": KEEP_BYTES}

        def carve(kind, key, ncols, dt):
            nbytes = ncols * (2 if dt == BF16 else 4)
            s = off[kind]
            off[kind] += nbytes
            assert off[kind] <= (KEEP_BYTES if kind == "keep" else UNI_BYTES), (kind, key, off[kind])
            P.region(key, s, s + nbytes)
            coff[key] = s
            v = uni[:, s // 4:(s + nbytes) // 4]
            return v.bitcast(BF16) if dt == BF16 else v

        coff = {}

        def carve_at(key, s, ncols, dt):
            nbytes = ncols * (2 if dt == BF16 else 4)
            P.region(key, s, s + nbytes)
            v = uni[:, s // 4:(s + nbytes) // 4]
            return v.bitcast(BF16) if dt == BF16 else v

        AO = []
        for S in range(2):
            d = {}
            for nm in ("qr", "kr", "vr", "vg"):
                d[nm] = carve("keep", "%s_%d" % (nm, S), 512, BF16)
            for nm in ("sgr", "sgg"):
                d[nm] = carve("keep", "%s_%d" % (nm, S), 512, F32)
            for nm in ("qg", "kg", "kh"):
                d[nm] = carve("keep", "%s_%d" % (nm, S), 256, BF16)
            AO.append(d)
        junk = carve("mix", "junk", D, BF16)
        xn = carve("mix", "xn", D, BF16)
        xT_main = carve("mix", "xT", D, BF16).rearrange("p (k c) -> p k c", c=128)
        rot = [carve("mix", "rot%d" % i, 256, F32).rearrange("p (h c) -> p h c", c=64) for i in range(4)]
        glb = carve("mix", "glb", 16, BF16)
        e1 = carve("mix", "e1", 256, F32); nls = carve("mix", "nls", 256, F32)
        eq = carve("mix", "eq", 256, F32); ek = carve("mix", "ek", 256, F32); erc_main = carve("mix", "erc", 256, F32)
        qkT = carve("mix", "qkT", 1024, BF16).rearrange("p (k c) -> p k c", c=128)
        scm = carve("mix", "scm", 512, BF16).rearrange("p (h c) -> p h c", c=128)
        dsc = carve("mix", "dsc", 512, F32).rearrange("p (h c) -> p h c", c=128)
        qkgT = carve("mix", "qkgT", 1024, BF16).rearrange("p (k c) -> p k c", c=128)
        nr = carve("mix", "nr", 512, F32); ng = carve("mix", "ng", 512, F32)
        mixed = carve("mix", "mixed", D, BF16)
        mT = carve("mix", "mT", D, BF16).rearrange("p (k c) -> p k c", c=128)
        hn = carve("mix", "hn", D, BF16)
        cs_main = carve("mix", "cs", 128, F32); ckf = carve("mix", "ckf", 128, F32); ckm = carve("mix", "ckm", 128, F32)
        cki = carve("mix", "cki", 128, F32).bitcast(I32)
        xT_alt1 = carve_at("xT_alt1", coff["mixed"], D, BF16).rearrange("p (k c) -> p k c", c=128)
        xT_alt2 = carve_at("xT_alt2", coff["mT"], D, BF16).rearrange("p (k c) -> p k c", c=128)
        erc_alt = carve_at("erc_alt", coff["nr"], 256, F32)
        cs_alt1 = carve_at("cs_alt1", coff["nr"] + 1024, 128, F32)
        cs_alt2 = carve_at("cs_alt2", coff["nr"] + 1536, 128, F32)
        NACC = 6
        acc = [carve("ffn", "acc%d" % i, NT * 128, F32) for i in range(NACC)]
        actT = [carve("ffn", "actT%d" % g, NT * 128, BF16) for g in range(NG)]
        dumr = T("dumr", [128, 512], BF16)
        corr = T("corr", [128, 2 * NG, 2], F32)
        ctmp = T("ctmp", [128, 2 * NG], F32)

        if os.environ.get("KDEBUG"):
            print("SBUF bytes/partition:", sb_bytes[0])
        for i in range(3):
            P.new_dma_sem("xld%d" % i)
        P.new_dma_sem("su", total=10)
        P.new_dma_sem("su2", total=11)
        for i in range(12):
            P.new_dma_sem("stg%d" % i)
        P.new_dma_sem("sucB", total=26)
        P.new_dma_sem("sucf", total=38)
        su = lambda out, in_, key, slow=False: P.dma("sp", "su", out, in_, writes=[key], slow=slow)
        su2 = lambda out, in_, key, slow=False: P.dma("sp", "su2", out, in_, writes=[key], slow=slow)
        su(identf[:], ident_d, "identf"); su(cvec[:], cvec_d, "cvec")
        su(rotc[:, 0, :], rot_d[0:1, :].partition_broadcast(128), "rotc0"); su(rotc[:, 1, :], rot_d[1:2, :].partition_broadcast(128), "rotc1")
        su(posi[:], pos_d, "posi")
        su(n1wT[:], n1w_d.rearrange("(k p) -> p k", p=128), "n1wT", slow=True)
        su(gw2f[0:16, :], gw2_d, "gw2fa"); su(gw2f[16:17, :], gb_d, "gw2fb")
        su(triu[:], triu_d, "triu"); su(tril[:], tril_d, "tril")
        P.dma("sp", "xld0", xb[0][:], xs_d[0:128, :], writes=["xb0"])
        su2(maskr[:], maskr_d, "maskr"); su2(maskg[:], maskg_d, "maskg")
        su2(fnw[:], fnw_d.partition_broadcast(128), "fnw")
        su2(rnw[:], rnw_d.partition_broadcast(128), "rnw"); su2(rnb[:], rnb_d.partition_broadcast(128), "rnb"); su2(gnw[:], gnw_d.partition_broadcast(128), "gnw")
        su2(n2wT[:], n2w_d.rearrange("(k p) -> p k", p=128), "n2wT", slow=True)
        for i in range(3):
            su2(cwt[:, i, :], cw_d[i].rearrange("(g p) -> p g", p=128), "cwt%d" % i, slow=True)
        su2(cwt[:, 3, :], cb_d.rearrange("(g p) -> p g", p=128), "cwt3", slow=True)
        win_v = win_d.rearrange("(k p) c -> p k c", p=128)
        wout_v = wout_d.rearrange("(k p) c -> p k c", p=128)
        RA = [(C_RK, C_RG), (C_GK, C_GG), (C_GL, DIN)]
        RB = [(C_RQ, C_RK), (C_RG, C_GK), (C_GG, C_GL)]
        WINA = ["winA%d_%d" % (k, i) for k in range(8) for i in range(3)]
        WINB = ["winB%d_%d" % (k, i) for k in range(8) for i in range(3)]
        WIN = WINA + WINB
        WOUT = ["wout0", "wout1"]
        WUPS = ["wup_s%d" % k for k in range(16)]
        WDNS = ["wdn_s%d" % k for k in range(22)]
        stage = [uni[:, KEEP_BYTES // 4 + sl * 1808:KEEP_BYTES // 4 + (sl + 1) * 1808] for sl in range(4)]
        for sl in range(4):
            P.region("stage%d" % sl, KEEP_BYTES + sl * 1808 * 4, KEEP_BYTES + (sl + 1) * 1808 * 4)
        soff = [0, 1024, 1792]
        for sl in range(4):
            for i, (c0, c1) in enumerate([(C_RK, C_RG), (C_GK, C_GG), (C_GL, DIN)]):
                b0 = KEEP_BYTES + sl * 1808 * 4 + soff[i] * 4
                P.region("stage%d_%d" % (sl, i), b0, b0 + (c1 - c0) * 4)
        for k in range(8):
            sl = k % 4
            for i, (c0, c1) in enumerate(RA):
                n_ = c1 - c0
                P.dma("sp", "stg%d" % (sl * 3 + i), stage[sl][:, soff[i]:soff[i] + n_], win_v[:, k, c0:c1], writes=["stage%d_%d" % (sl, i)])
            for i, (c0, c1) in enumerate(RA):
                n_ = c1 - c0
                eng_ = ("dve", "act", "pool")[i]
                if eng_ == "act":
                    P.op("act", lambda e, k=k, sl=sl, i=i, c0=c0, c1=c1, n_=n_: e.copy(win[:, k, c0:c1], stage[sl][:, soff[i]:soff[i] + n_]),
                         ["stage%d_%d" % (sl, i)], ["winA%d_%d" % (k, i)], 0.9)
                else:
                    P.op(eng_, lambda e, k=k, sl=sl, i=i, c0=c0, c1=c1, n_=n_: e.tensor_copy(win[:, k, c0:c1], stage[sl][:, soff[i]:soff[i] + n_]),
                         ["stage%d_%d" % (sl, i)], ["winA%d_%d" % (k, i)], 1.0)
        bulk = []
        for k in range(16):
            bulk.append(("sucf", wup_s[64 * k:64 * k + 64, :], wup_d[64 * k:64 * k + 64, :], WUPS[k]))
        for k in range(8):
            for i, (c0, c1) in enumerate(RB):
                bulk.append(("sucB", win[:, k, c0:c1], win_v[:, k, c0:c1], "winB%d_%d" % (k, i)))
        for k in range(2):
            bulk.append(("sucB", wout[:, 4 * k:4 * k + 4, :], wout_v[:, 4 * k:4 * k + 4, :], WOUT[k]))
        for k in range(22):
            bulk.append(("sucf", wdn_s[128 * k:128 * k + 128, :], wdn_d[128 * k:128 * k + 128, :], WDNS[k]))

        def issue_bulk(n):
            for _ in range(min(n, len(bulk))):
                sem_, o_, i_, key_ = bulk.pop(0)
                P.dma("pool", sem_, o_, i_, writes=[key_])
        wup_sv = wup_s.rearrange("(k p) c -> p k c", p=128)
        wdn_sv = wdn_s.rearrange("(k p) c -> p k c", p=128)

        dve = lambda fn, r, w, c=None: P.op("dve", fn, r, w, c)
        act = lambda fn, r, w, c=None: P.op("act", fn, r, w, c)
        pool = lambda fn, r, w, c=None: P.op("pool", fn, r, w, c)
        pe = lambda fn, r, w, c=None: P.op("pe", fn, r, w, c)

        dve(lambda e: e.tensor_copy(ident[:], identf[:]), ["identf"], ["ident"])
        dve(lambda e: e.tensor_copy(gw2b[:], gw2f[:]), ["gw2fa", "gw2fb"], ["gw2b"])
        dve(lambda e: e.tensor_copy(posr[:], posi[:]), ["posi"], ["posr"])
        pe(lambda e: e.matmul(bank[3][:, 0:NV], posr[:], identf[0:NV, 0:NV], start=True, stop=True), ["posr", "identf"], [bkey(3)])
        dve(lambda e: e.tensor_copy(posf[:], bank[3][:, 0:NV]), [bkey(3)], ["posf"])
        dve(lambda e: e.memset(Sr[:], 0.0), [], ["Sr"])
        dve(lambda e: e.memset(Sg[:], 0.0), [], ["Sg"])
        dve(lambda e: e.memset(Srb[:], 0.0), [], ["Srb"])
        dve(lambda e: e.memset(Sgb[:], 0.0), [], ["Sgb"])
        dve(lambda e: e.memset(carry[:], 0.0), [], ["carry%d" % c for c in range(2 * NG)])
        dve(lambda e: e.memset(glT[:], 1.0), [], ["glT"])
        dve(lambda e: e.memset(dumr[:], 0.5), [], ["dumr"])
        fill_i = [0]

        def filler():
            b_ = (5, 7)[fill_i[0] % 2]
            fill_i[0] += 1
            P.op("pe", lambda e, b_=b_: e.matmul(bank[b_][:], ident[:], dumr[:], start=True, stop=True), ["ident", "dumr"], ["fillbank%d" % b_], 0.22)

        for i in range(NT):
            P.new_dma_sem("yst%d" % i)
        for i in range(NWU):
            P.new_dma_sem("wul%d_0" % i)
            P.new_dma_sem("wul%d_1" % i)

        def load_x(t):
            P.dma("sp", "xld%d" % (t % 3), xb[t % 3][:], xs_d[t * 128:(t + 1) * 128, :], writes=["xb%d" % (t % 3)])

        def rms_and_transpose(src, srckey, ss_col, nwT, nwkey, dstT, dstkeys, xn_t, xnkey, tb):
            ss = sm[:, ss_col:ss_col + 1]
            rs = sm[:, ss_col + 1:ss_col + 2]
            act(lambda e: e.activation(junk, src, AF.Square, accum_out=ss), [srckey], ["ss%d" % ss_col, "junk"], 1.1)
            pool(lambda e: e.tensor_scalar(rs, ss, 1.0 / D, EPS, ALU.mult, ALU.add), ["ss%d" % ss_col], ["rs%d" % ss_col], 0.2)
            pool(lambda e: e.tensor_tensor(rs, rs, cvec[:, 12:13], ALU.pow), ["rs%d" % ss_col, "cvec"], ["rs%d" % ss_col], 0.5)
            dve(lambda e: e.tensor_scalar(xn_t, src, rs, None, ALU.mult), [srckey, "rs%d" % ss_col], [xnkey], 0.8)
            for k in range(8):
                pe(lambda e, k=k: e.transpose(bf(tb)[:, k, :], xn_t[:, k * 128:(k + 1) * 128], ident[:]), [xnkey, "ident"], [bkey(tb)], 0.1)
            dve(lambda e: e.tensor_tensor(dstT, bf(tb), nwT[:].unsqueeze(2).to_broadcast([128, 8, 128]), ALU.mult), [bkey(tb), nwkey], dstkeys, 1.25)

        pj_i = [0]

        def proj_block(lhsT_t, lhskey, w_t, wkey, c0, ncol, banks=(1, 2)):
            b = banks[pj_i[0] % 2]
            pj_i[0] += 1
            for k in range(8):
                pe(lambda e, k=k, b=b: e.matmul(bank[b][:, 0:ncol], lhsT_t[:, k, :], w_t[:, k, c0:c0 + ncol], start=(k == 0), stop=(k == 7)),
                   [lhskey] + wkey, [bkey(b)], 0.27 if ncol > 256 else (0.16 if ncol > 16 else 0.07))
            return b

        def rotary(b, dst, dstkey, cs, CS):
            ps = bank[b][:].rearrange("p (h t c) -> p h t c", h=4, t=2)
            cosb = cs[:, 64:128].unsqueeze(1).to_broadcast([128, 4, 64])
            sinb = cs[:, 0:64].unsqueeze(1).to_broadcast([128, 4, 64])
            d4 = dst.rearrange("p (h t c) -> p h t c", h=4, t=2)
            dve(lambda e: e.tensor_tensor(rot[0], ps[:, :, 0, :], cosb, ALU.mult), [bkey(b), CS], ["rot0"], 0.37)
            dve(lambda e: e.tensor_tensor(rot[1], ps[:, :, 1, :], sinb, ALU.mult), [bkey(b), CS], ["rot1"], 0.37)
            dve(lambda e: e.tensor_tensor(rot[2], ps[:, :, 1, :], cosb, ALU.mult), [bkey(b), CS], ["rot2"], 0.37)
            dve(lambda e: e.tensor_tensor(rot[3], ps[:, :, 0, :], sinb, ALU.mult), [bkey(b), CS], ["rot3"], 0.37)
            pool(lambda e: e.tensor_tensor(d4[:, :, 0, :], rot[0], rot[1], ALU.subtract), ["rot0", "rot1"], [dstkey + "a"], 0.73)
            pool(lambda e: e.tensor_tensor(d4[:, :, 1, :], rot[2], rot[3], ALU.add), ["rot2", "rot3"], [dstkey + "b"], 0.73)

        def stage_a(t, full):
            S = t % 2
            o = AO[S]
            K_ = lambda nm: "%s_%d" % (nm, S)
            xk = "xb%d" % (t % 3)
            x_t = xb[t % 3]
            E3 = t % 3
            edec = sm[0:64, 8 + 4 * E3:12 + 4 * E3] if E3 < 2 else sm[0:64, 60:64]
            KE = "edec_%d" % E3
            q3 = t % 3 if not full else 0
            xT, XT = [(xT_main, "xT"), (xT_alt1, "xT_alt1"), (xT_alt2, "xT_alt2")][q3]
            cs, CS = [(cs_main, "cs"), (cs_alt1, "cs_alt1"), (cs_alt2, "cs_alt2")][q3]
            use_alt = (not full) and (t % 2 == 0)
            erc, ERC = (erc_alt, "erc_alt") if use_alt else (erc_main, "erc")
            pj_i[0] = 0
            WIN = (WINA + WINB) if full else WINA
            dve(lambda e: e.scalar_tensor_tensor(cs, rotc[:, 0, :], posf[:, t:t + 1], rotc[:, 1, :], ALU.mult, ALU.add), ["rotc0", "rotc1", "posf"], [CS], 0.27)
            dve(lambda e: e.tensor_copy(cki, cs), [CS], ["cki"], 0.27)
            dve(lambda e: e.tensor_copy(ckf, cki), ["cki"], ["ckf"], 0.27)
            dve(lambda e: e.tensor_tensor(cs, cs, ckf, ALU.subtract), [CS, "ckf"], [CS], 0.27)
            dve(lambda e: e.tensor_scalar(ckm, cs, 0.5, None, ALU.is_gt), [CS], ["ckm"], 0.27)
            dve(lambda e: e.tensor_tensor(cs, cs, ckm, ALU.subtract), [CS, "ckm"], [CS], 0.27)
            act(lambda e: e.activation(cs, cs, AF.Sin, scale=2 * math.pi), [CS], [CS], 0.3 if full else 1.6)

            rms_and_transpose(x_t[:], xk, 0, n1wT, "n1wT", xT, [XT], xn, "xn", 0)
            P.cut()

            b = 3
            for k in range(8):
                pe(lambda e, k=k: e.matmul(bank[3][:, 128:144], xT[:, k, :], win[:, k, C_GL:C_GL + 16], start=(k == 0), stop=(k == 7)), [XT] + WIN, [bkey(3)], 0.07)
            dve(lambda e: e.tensor_copy(glb, bank[3][:, 128:144]), [bkey(3)], ["glb"], 0.2)
            pe(lambda e: e.transpose(bank[3][:].bitcast(BF16)[0:16, 0:128], glb, ident[:]), ["glb", "ident"], [bkey(3)], 0.1)
            dve(lambda e: e.tensor_copy(glT[0:16, :], bank[3][:].bitcast(BF16)[0:16, 0:128]), [bkey(3)], ["glT"], 0.2)
            pe(lambda e: e.matmul(bank[3][:, 256:512], glT[:], gw2b[:], start=True, stop=True), ["glT", "gw2b"], [bkey(3)], 0.3)
            act(lambda e: e.activation(e1, bank[3][:, 256:512], AF.Exp, scale=-1.0), [bkey(3)], ["e1"], 1.7)
            act(lambda e: e.activation(nls, e1, AF.Ln, bias=1.0), ["e1"], ["nls"], 0.42)
            if full:
                pe(lambda e: e.matmul(bank[0][:, 0:256], triu[:], nls, start=True, stop=True), ["triu", "nls"], [bkey(0)], 0.55)
            cb = 0 if full else 5
            if not full:
                P.fill(int(os.environ.get("FILL_RC", "12")))
            pe(lambda e: e.matmul(bank[cb][:, 256:512], tril[:], nls, start=True, stop=True), ["tril", "nls"], [bkey(cb)], 0.55)
            for h in range(4):
                pe(lambda e, h=h: e.matmul(bank[3][0:64, 64 + h:65 + h], nls[:, h * 64:(h + 1) * 64], cvec[:, 13:14], start=True, stop=True),
                   ["nls", "cvec"], [bkey(3)], 0.1)
            if full:
                act(lambda e: e.activation(eq, bank[0][:, 0:256], AF.Exp, scale=-1.0 / 16), [bkey(0)], ["eq"], 0.35)
                act(lambda e: e.activation(ek, bank[0][:, 0:256], AF.Exp, scale=1.0 / 16), [bkey(0)], ["ek"], 0.3)
            act(lambda e: e.activation(erc, bank[cb][:, 256:512], AF.Exp, scale=-1.0 / 16), [bkey(cb)], [ERC], 0.3)
            act(lambda e: e.activation(edec, bank[3][0:64, 64:68], AF.Exp, scale=-1.0 / 16), [bkey(3)], [KE], 0.2)

            if full:
                b = proj_block(xT, XT, win, WIN, C_GQ, 512, banks=(0, 0))
                dve(lambda e, b=b: e.scalar_tensor_tensor(o["qg"], bank[b][:, 0:256], 0.125, eq, ALU.mult, ALU.mult), [bkey(b), "eq"], [K_("qg")], 0.42)
                dve(lambda e, b=b: e.tensor_tensor(o["kg"], bank[b][:, 256:512], ek, ALU.mult), [bkey(b), "ek"], [K_("kg")], 0.35)
                dve(lambda e, b=b: e.tensor_tensor(o["kh"], bank[b][:, 256:512], erc, ALU.mult), [bkey(b), ERC], [K_("kh")], 0.35)
            P.cut()
            pj_i[0] = 0
            if not full:
                b = proj_block(xT, XT, win, WIN, C_GK, 256)
                dve(lambda e, b=b: e.tensor_tensor(o["kh"], bank[b][:, 0:256], erc, ALU.mult), [bkey(b), ERC], [K_("kh")], 0.35)
            if full:
                b = proj_block(xT, XT, win, WIN, C_RQ, 512)
                rotary(b, o["qr"], K_("qr"), cs, CS)
            b = proj_block(xT, XT, win, WIN, C_RV, 512)
            for h in range(4):
                act(lambda e, h=h, b=b: e.activation(o["vr"][:, h * 128:(h + 1) * 128], bank[b][:, h * 128:(h + 1) * 128], AF.Identity, scale=cvec[:, h:h + 1]),
                    [bkey(b), "cvec"], [K_("vr")], 0.37)
            b = proj_block(xT, XT, win, WIN, C_RK, 512)
            rotary(b, o["kr"], K_("kr"), cs, CS)
            b = proj_block(xT, XT, win, WIN, C_GV, 512)
            act(lambda e, b=b: e.copy(o["vg"], bank[b][:]), [bkey(b)], [K_("vg")])
            if full:
                b = proj_block(xT, XT, win, WIN, C_RG, 512)
                act(lambda e, b=b: e.activation(o["sgr"], bank[b][:], AF.Silu), [bkey(b)], [K_("sgr")], 1.9)
                b = proj_block(xT, XT, win, WIN, C_GG, 512)
                act(lambda e, b=b: e.activation(o["sgg"], bank[b][:], AF.Silu), [bkey(b)], [K_("sgg")])

        def stage_b(t, full, slot):
            S = t % 2
            o = AO[S]
            K_ = lambda nm: "%s_%d" % (nm, S)
            xk = "xb%d" % (t % 3)
            x_t = xb[t % 3]
            E3 = t % 3
            edec = sm[0:64, 8 + 4 * E3:12 + 4 * E3] if E3 < 2 else sm[0:64, 60:64]
            KE = "edec_%d" % E3
            qr, kr, vr, vg, qg, kg, kh, sgr, sgg = (o[n] for n in ("qr", "kr", "vr", "vg", "qg", "kg", "kh", "sgr", "sgg"))
            KR = [K_("kr") + "a", K_("kr") + "b"]
            QR = [K_("qr") + "a", K_("qr") + "b"]
            if full:
                pool(lambda e: e.tensor_tensor(ng, rnb[:], sgr, ALU.mult), ["rnb", K_("sgr")], ["ng"], 1.26)
                pool(lambda e: e.tensor_tensor(sgr, rnw[:], sgr, ALU.mult), ["rnw", K_("sgr")], [K_("sgr")], 1.26)
                pool(lambda e: e.tensor_tensor(sgg, gnw[:], sgg, ALU.mult), ["gnw", K_("sgg")], [K_("sgg")], 1.26)
            if full:
                for h in range(4):
                    pe(lambda e, h=h: e.transpose(bf(5)[:, h, :], qr[:, h * 128:(h + 1) * 128], ident[:]), QR + ["ident"], [bkey(5)], 0.1)
                    pe(lambda e, h=h: e.transpose(bf(5)[:, 4 + h, :], kr[:, h * 128:(h + 1) * 128], ident[:]), KR + ["ident"], [bkey(5)], 0.1)
                act(lambda e: e.copy(qkT, bf(5)), [bkey(5)], ["qkT"], 0.9)
                for h in range(4):
                    pe(lambda e, h=h: e.matmul(f3(6)[:, h, :], qkT[:, 4 + h, :], qkT[:, h, :], start=True, stop=True), ["qkT"], [bkey(6)], 0.1)
                dve(lambda e: e.tensor_tensor(scm, f3(6), maskr[:].unsqueeze(1).to_broadcast([128, 4, 128]), ALU.mult), [bkey(6), "maskr"], ["scm"], 0.7)
                for h in range(4):
                    pe(lambda e, h=h: e.matmul(f3(7)[:, h, :], scm[:, h, :], vr[:, h * 128:(h + 1) * 128], start=True, stop=False), ["scm", K_("vr")], [bkey(7)], 0.1)
                    pe(lambda e, h=h: e.matmul(f3(7)[:, h, :], qkT[:, h, :], Srb[:, h, :], start=False, stop=True), ["qkT", "Srb"], [bkey(7)], 0.1)
            for h in range(4):
                pe(lambda e, h=h: e.matmul(f3(4)[:, h, :], kr[:, h * 128:(h + 1) * 128], vr[:, h * 128:(h + 1) * 128], start=True, stop=True),
                   KR + [K_("vr")], [bkey(4)], 0.1)
            for h in range(4):
                gc = GAM[h] ** 128
                act(lambda e, h=h, gc=gc: e.activation(dsc[:, h, :], f3(4)[:, h, :], AF.Identity, scale=gc / math.sqrt(128.0)), [bkey(4)], ["dsc"], 0.3)
            for h in range(4):
                gc = GAM[h] ** 128
                dve(lambda e, h=h, gc=gc: e.scalar_tensor_tensor(Sr[:, h, :], Sr[:, h, :], gc, dsc[:, h, :], ALU.mult, ALU.add), ["Sr", "dsc"], ["Sr"], 0.3)
            act(lambda e: e.copy(Srb[:], Sr[:]), ["Sr"], ["Srb"], 0.6)
            P.cut()

            if full:
                for h in range(4):
                    pe(lambda e, h=h: e.transpose(bf(5)[0:64, h, :], qg[:, h * 64:(h + 1) * 64], ident[:]), [K_("qg"), "ident"], [bkey(5)], 0.1)
                    pe(lambda e, h=h: e.transpose(bf(5)[0:64, 4 + h, :], kg[:, h * 64:(h + 1) * 64], ident[:]), [K_("kg"), "ident"], [bkey(5)], 0.1)
                act(lambda e: e.copy(qkgT[0:64], bf(5)[0:64]), [bkey(5)], ["qkgT"], 0.9)
                for h in range(4):
                    pe(lambda e, h=h: e.matmul(f3(6)[:, h, :], qkgT[0:64, 4 + h, :], qkgT[0:64, h, :], start=True, stop=True), ["qkgT"], [bkey(6)], 0.1)
                dve(lambda e: e.tensor_tensor(scm, f3(6), maskg[:].unsqueeze(1).to_broadcast([128, 4, 128]), ALU.mult), [bkey(6), "maskg"], ["scm"], 0.7)
                for h in range(4):
                    pe(lambda e, h=h: e.matmul(f3(4)[:, h, :], scm[:, h, :], vg[:, h * 128:(h + 1) * 128], start=True, stop=False), ["scm", K_("vg")], [bkey(4)], 0.1)
                    pe(lambda e, h=h: e.matmul(f3(4)[:, h, :], qkgT[0:64, h, :], Sgb[:, h, :], start=False, stop=True), ["qkgT", "Sgb"], [bkey(4)], 0.1)
            for h in range(4):
                pe(lambda e, h=h: e.matmul(f3(6)[0:64, h, :], kh[:, h * 64:(h + 1) * 64], vg[:, h * 128:(h + 1) * 128], start=True, stop=True), [K_("kh"), K_("vg")], [bkey(6)], 0.1)
            for h in range(4):
                dve(lambda e, h=h: e.scalar_tensor_tensor(Sg[:, h, :], Sg[:, h, :], edec[:, h:h + 1], f3(6)[0:64, h, :], ALU.mult, ALU.add),
                    ["Sg", KE, bkey(6)], ["Sg"], 0.3)
            act(lambda e: e.copy(Sgb[:], Sg[:]), ["Sg"], ["Sgb"], 0.45)
            P.cut()
            if not full:
                return

            st6 = sm[:, 16:40].rearrange("p (h c) -> p h c", c=6)
            mv = sm[:, 40:48].rearrange("p (h c) -> p h c", c=2)
            for h in range(4):
                dve(lambda e, h=h: e.bn_stats(st6[:, h, :], f3(7)[:, h, :]), [bkey(7)], ["st6"], 0.3)
            for h in range(4):
                dve(lambda e, h=h: e.bn_aggr(mv[:, h, :], st6[:, h, :]), ["st6"], ["mv"], 0.18)
            rsr = sm[:, 48:52]
            dve(lambda e: e.tensor_tensor(rsr, mv[:, :, 1], cvec[:, 8:12], ALU.mult), ["mv", "cvec"], ["rsr"], 0.17)
            dve(lambda e: e.tensor_scalar(rsr, rsr, EPS, None, ALU.add), ["rsr"], ["rsr"], 0.17)
            pool(lambda e: e.tensor_tensor(rsr, rsr, cvec[:, 12:13].to_broadcast([128, 4]), ALU.pow), ["rsr", "cvec"], ["rsr"], 1.0)
            dve(lambda e: e.tensor_tensor(rsr, rsr, cvec[:, 4:8], ALU.mult), ["rsr", "cvec"], ["rsr"], 0.17)
            for h in range(4):
                dve(lambda e, h=h: e.tensor_scalar(nr[:, h * 128:(h + 1) * 128], f3(7)[:, h, :], mv[:, h, 0:1], rsr[:, h:h + 1], ALU.subtract, ALU.mult),
                    [bkey(7), "mv", "rsr"], ["nr"], 0.41)
            dve(lambda e: e.tensor_tensor(nr, nr, sgr, ALU.mult), ["nr", K_("sgr")], ["nr"], 0.69)
            dve(lambda e: e.tensor_tensor(mixed[:, 0:512], nr, ng, ALU.add), ["nr", "ng"], ["mixeda"], 0.69)
            P.cut()
            ssg = sm[:, 52:56]
            rsg = sm[:, 56:60]
            for h in range(4):
                act(lambda e, h=h: e.activation(junk[:, h * 128:(h + 1) * 128], f3(4)[:, h, :], AF.Square, accum_out=ssg[:, h:h + 1]), [bkey(4)], ["ssg", "junk"], 0.37)
            pool(lambda e: e.tensor_scalar(rsg, ssg, 1.0 / 128, EPS, ALU.mult, ALU.add), ["ssg"], ["rsg"], 0.2)
            pool(lambda e: e.tensor_tensor(rsg, rsg, cvec[:, 12:13].to_broadcast([128, 4]), ALU.pow), ["rsg", "cvec"], ["rsg"], 1.0)
            for h in range(4):
                dve(lambda e, h=h: e.scalar_tensor_tensor(mixed[:, 512 + h * 128:512 + (h + 1) * 128], f3(4)[:, h, :], rsg[:, h:h + 1], sgg[:, h * 128:(h + 1) * 128], ALU.mult, ALU.mult),
                    [bkey(4), "rsg", K_("sgg")], ["mixedb"], 0.35)
            P.cut()
            for k in range(8):
                pe(lambda e, k=k: e.transpose(bf(5)[:, k, :], mixed[:, k * 128:(k + 1) * 128], ident[:]), ["mixeda", "mixedb", "ident"], [bkey(5)], 0.1)
            act(lambda e: e.copy(mT, bf(5)), [bkey(5)], ["mT"], 0.9)
            hk = "hb%d" % slot
            pj_b = [0]
            for n in range(2):
                b = (6, 7)[n]
                for k in range(8):
                    pe(lambda e, k=k, b=b, n=n: e.matmul(bank[b][:], mT[:, k, :], wout[:, k, n * 512:(n + 1) * 512], start=(k == 0), stop=(k == 7)),
                       ["mT"] + WOUT, [bkey(b)])
                dve(lambda e, b=b, n=n: e.tensor_tensor(hb[slot][:, n * 512:(n + 1) * 512], x_t[:, n * 512:(n + 1) * 512], bank[b][:], ALU.add),
                    [bkey(b), xk], [hk], 0.69)
            rms_and_transpose(hb[slot][:], hk, 2, n2wT, "n2wT", hnT[:, :, slot * 128:(slot + 1) * 128], ["hnT%d" % slot], hn, "hn", 5)

        wu_i = [0]
        wd_i = [0]
        CAR = ["carry%d" % c for c in range(2 * NG)]

        pref = []

        def up_load(blk):
            s = wu_i[0] % NWU
            wu_i[0] += 1
            P.dma("sp", "wul%d_0" % s, wu[s][:, 0], wup_sv[:, :, blk * 256:(blk + 1) * 256], reads=WUPS, writes=["wu%d_0" % s])
            P.dma("sp", "wul%d_1" % s, wu[s][:, 1], wup_sv[:, :, DFF + blk * 256:DFF + (blk + 1) * 256], reads=WUPS, writes=["wu%d_1" % s])
            return s

        def prefetch_up():
            del pref[:]
            for blk in range(NWU):
                pref.append(up_load(blk))

        def ffn(ntl, out_tile0):
            halo = out_tile0 < 0
            ntok = ntl * 128 if not halo else 2
            tok0 = 0 if not halo else 126
            hkeys = ["hnT%d" % j for j in range(ntl)]
            if not halo:
                pool(lambda e: e.tensor_tensor(corr[:, :, 0], carry[:, :, 0], cwt[:, 0, :], ALU.mult), CAR + ["cwt0"], ["corr"])
                pool(lambda e: e.tensor_tensor(ctmp[:], carry[:, :, 1], cwt[:, 1, :], ALU.mult), CAR + ["cwt1"], ["ctmp"])
                pool(lambda e: e.tensor_tensor(corr[:, :, 0], corr[:, :, 0], ctmp[:], ALU.add), ["corr", "ctmp"], ["corr"])
                pool(lambda e: e.tensor_tensor(corr[:, :, 1], carry[:, :, 1], cwt[:, 0, :], ALU.mult), CAR + ["cwt0"], ["corr"])
            pair_i = 0
            for blk in range(NG // 2):
                if blk < len(pref):
                    s = pref[blk]
                else:
                    s = up_load(blk)
                for gi in range(2):
                    g = blk * 2 + gi
                    pp = pair_i % 3
                    pair_i += 1
                    bg, bv = 2 * pp, 2 * pp + 1
                    ag, av = acc[2 * pp], acc[2 * pp + 1]
                    kag, kav = "acc%d" % (2 * pp), "acc%d" % (2 * pp + 1)
                    for half, bb_ in ((0, bg), (1, bv)):
                        for k in range(8):
                            pe(lambda e, k=k, half=half, bb_=bb_, s=s, gi=gi: e.matmul(bank[bb_][:, 0:ntok], wu[s][:, half, k, gi * 128:(gi + 1) * 128], hnT[:, k, tok0:tok0 + ntok],
                                                                                     start=(k == 0), stop=(k == 7)), hkeys + ["wu%d_%d" % (s, half)], [bkey(bb_)], 0.27 if not halo else 0.07)
                    for (a, ka, bb_, ch) in ((ag, kag, bg, g), (av, kav, bv, NG + g)):
                        if not halo:
                            act(lambda e, a=a, bb_=bb_, ch=ch: e.activation(a[:, 0:ntok], bank[bb_][:, 0:ntok], AF.Identity, bias=cwt[:, 3, ch:ch + 1], scale=cwt[:, 2, ch:ch + 1]),
                                [bkey(bb_), "cwt2", "cwt3"], [ka], 0.62)
                            dve(lambda e, a=a, bb_=bb_, ch=ch: e.scalar_tensor_tensor(a[:, 1:ntok], bank[bb_][:, 0:ntok - 1], cwt[:, 1, ch:ch + 1], a[:, 1:ntok], ALU.mult, ALU.add),
                                [bkey(bb_), ka, "cwt1"], [ka], 0.7)
                            dve(lambda e, a=a, bb_=bb_, ch=ch: e.scalar_tensor_tensor(a[:, 2:ntok], bank[bb_][:, 0:ntok - 2], cwt[:, 0, ch:ch + 1], a[:, 2:ntok], ALU.mult, ALU.add),
                                [bkey(bb_), ka, "cwt0"], [ka], 0.7)
                            pool(lambda e, a=a, ch=ch: e.tensor_tensor(a[:, 0:2], a[:, 0:2], corr[:, ch, :], ALU.add), [ka, "corr"], [ka], 0.2)
                        act(lambda e, bb_=bb_, ch=ch: e.copy(carry[:, ch, :], bank[bb_][:, ntok - 2:ntok]), [bkey(bb_), "corr"], ["carry%d" % ch], 0.25)
                    if halo:
                        continue
                    act(lambda e, a=ag: e.activation(a[:, 0:ntok], a[:, 0:ntok], AF.Silu), [kag], [kag], 0.62)
                    pool(lambda e, g=g, a=ag, a2=av: e.tensor_tensor(actT[g][:, 0:ntok], a[:, 0:ntok], a2[:, 0:ntok], ALU.mult), [kag, kav], ["actT%d" % g], 1.26)
            del pref[:]
            if halo:
                prefetch_up()
                return
            if os.environ.get("KDEBUG") and out_tile0 == 12:
                print("ffn up done", {k: round(v, 1) for k, v in P.t_eng.items()})
            dbank = [6, 7, 0, 1]
            for n in range(2):
                for k0 in range(0, NG, 8):
                    nk = min(8, NG - k0)
                    s = wu_i[0] % NWU
                    wu_i[0] += 1
                    wv = wu[s][:].rearrange("p a k c -> p (a k c)").rearrange("p (k c) -> p k c", c=512)
                    P.dma("sp", "wul%d_0" % s, wv[:, 0:nk, :], wdn_sv[:, k0:k0 + nk, n * 512:(n + 1) * 512], reads=WDNS, writes=["wu%d_0" % s, "wu%d_1" % s])
                    for kk in range(nk):
                        k = k0 + kk
                        for j in range(ntl):
                            pe(lambda e, k=k, kk=kk, j=j, wv=wv: e.matmul(bank[dbank[j]][:], actT[k][:, j * 128:(j + 1) * 128], wv[:, kk, :],
                                                                         start=(k == 0), stop=(k == NG - 1)), ["actT%d" % k, "wu%d_0" % s, "wu%d_1" % s], [bkey(dbank[j])])
                for j in range(ntl):
                    dve(lambda e, j=j, n=n: e.tensor_tensor(hb[j][:, n * 512:(n + 1) * 512], hb[j][:, n * 512:(n + 1) * 512], bank[dbank[j]][:], ALU.add),
                        [bkey(dbank[j]), "hb%d" % j], ["hb%d" % j], 0.69)
            if os.environ.get("KDEBUG") and out_tile0 == 12:
                print("ffn down done", {k: round(v, 1) for k, v in P.t_eng.items()})
            if out_tile0 + ntl < NMAIN:
                prefetch_up()
            P.record()
            for j in range(ntl):
                hk = "hb%d" % j
                ss = sm[:, 4:5]
                rs = sm[:, 5:6]
                act(lambda e, j=j: e.activation(junk, hb[j][:], AF.Square, accum_out=ss), [hk], ["ss4", "junk"], 1.1)
                pool(lambda e: e.tensor_scalar(rs, ss, 1.0 / D, EPS, ALU.mult, ALU.add), ["ss4"], ["rs4"], 0.2)
                pool(lambda e: e.tensor_tensor(rs, rs, cvec[:, 12:13], ALU.pow), ["rs4", "cvec"], ["rs4"], 0.5)
                dve(lambda e, j=j: e.scalar_tensor_tensor(hb[j][:], hb[j][:], rs, fnw[:], ALU.mult, ALU.mult), [hk, "rs4", "fnw"], [hk], 1.3)
                r0 = (out_tile0 + j) * 128
                P.dma("sp", "yst%d" % j, y_d[r0:r0 + 128, :], hb[j][:], reads=[hk])
            return P.stop()

        tiles = [(t, False, 0, None) for t in range(NPRE)]
        tiles.append((NPRE, True, 0, (1, -1)))
        for s in range(NSUP):
            for j in range(NT):
                tiles.append((NPRE + 1 + s * NT + j, True, j, (NT, s * NT) if j == NT - 1 else None))

        def rec_a(i):
            t, full, slot, f = tiles[i]
            P.record()
            if t + 1 < NV:
                load_x(t + 1)
            stage_a(t, full)
            return P.stop()

        def rec_b(i):
            t, full, slot, f = tiles[i]
            P.record()
            stage_b(t, full, slot)
            return P.stop()

        def stg_a(i):
            t, full, slot, f = tiles[i]
            P.record()
            if t + 1 < NV:
                load_x(t + 1)
            stage_a(t, full)
            c = P.stop_chunks()
            if full or not SPLIT:
                return [], [], [[c[0]], [c[1], c[2]]]
            return [[c[0]]], [[c[1]]], [[c[2]]]

        def stg_b(i):
            t, full, slot, f = tiles[i]
            P.record()
            stage_b(t, full, slot)
            c = P.stop_chunks()
            if not full:
                return [[c[0]], [c[1]]]
            return [[c[0]], [c[1], c[2]], [c[3]], [c[4]]]

        PIPE = int(os.environ.get("PIPE", "3"))
        SPLIT = bool(int(os.environ.get("SPLIT", "1")))
        FILL = bool(int(os.environ.get("FILL", "0")))
        assert NPRE % 2 == 1
        tails = {}
        BULK_PER = (len(bulk) + max(1, NPRE // 2) - 1) // max(1, NPRE // 2)
        stA = {}
        pend = [None]
        if PIPE == 3:
            stA[0] = stg_a(0)
            stA[1] = stg_a(1)
            if stA[0][0]:
                P.play_stages([stA[0][0]])
            P.play_stages([st_ for st_ in (stA[0][1], stA[1][0]) if st_])
        else:
            P.play(rec_a(0))
        for i in range(len(tiles) + (1 if PIPE == 3 else 0)):
            if PIPE != 3:
                ra = rec_a(i + 1) if i + 1 < len(tiles) else []
                rb = rec_b(i)
            if PIPE == 3:
                P.filler = filler if (FILL and i < NPRE - 1) else None
                issue_bulk(BULK_PER if i < NPRE - 2 else len(bulk))
                if i == NPRE - 2:
                    prefetch_up()
                if i + 2 < len(tiles):
                    stA[i + 2] = stg_a(i + 2)
                h1 = stA[i + 2][0] if i + 2 in stA else []
                h2 = stA[i + 1][1] if i + 1 in stA else []
                tl = stA[i][2] if i in stA else []
                sb_ = stg_b(i - 1) if i >= 1 else []
                if pend[0]:
                    P.play_stages([st_ for st_ in (sb_[:-1], tl, h2, h1, [[pend[0]]]) if st_])
                    P.play_stages([[sb_[-1]]] if sb_ else [])
                    pend[0] = None
                else:
                    P.play_stages([st_ for st_ in (sb_, tl, h2, h1) if st_])
                stA.pop(i, None)
                if os.environ.get("KDEBUG") and (i in (8, 9, 10, 11) or 44 <= i <= 46):
                    print("after step", i, {k: round(v, 1) for k, v in P.t_eng.items()})
                if i >= 1 and tiles[i - 1][3] is not None:
                    pend[0] = ffn(*tiles[i - 1][3])
                continue
            if False:
                if os.environ.get("KDEBUG") and (44 <= i <= 48 or i in (5, 6, 7)):
                    print("after tile", i, {k: round(v, 1) for k, v in P.t_eng.items()})
                if tiles[i][3] is not None:
                    ffn(*tiles[i][3])
                continue
            if PIPE == 2:
                P.play_merged(rb, ra)
            elif PIPE == 1:
                P.play(interleave(ra, rb))
            else:
                P.play(rb + ra)
            if os.environ.get("KDEBUG") and 44 <= i <= 53:
                print("after tile", i, {k: round(v, 1) for k, v in P.t_eng.items()})
            if tiles[i][3] is not None:
                ffn(*tiles[i][3])
                if os.environ.get("KDEBUG") and 44 <= i <= 53:
                    print("after ffn", i, {k: round(v, 1) for k, v in P.t_eng.items()})
        if pend[0]:
            P.play(pend[0])
        for j in range(NT):
            ent = P.dma_sems["yst%d" % j]
            P.streams["sp"].append(("wait", ent[0], ent[1]))
        if os.environ.get("KDEBUG"):
            print("fillers:", P.n_fill)
            print("MODEL est total us:", round(max(P.t_eng.values()), 1), {k: round(v, 1) for k, v in P.t_eng.items()}, "n_ops", P.cnt)
        P.emit()
    return nc


def _consts():
    l = np.arange(128, dtype=np.float64)
    c = {}
    c["c_ident"] = np.eye(128, dtype=np.float32)
    le = (l[:, None] <= l[None, :])
    c["c_maskr"] = (le / math.sqrt(128.0)).astype(np.float32)
    c["c_maskg"] = le.astype(np.float32)
    c["c_triu"] = le.astype(np.float32)
    c["c_tril"] = (l[:, None] > l[None, :]).astype(np.float32)
    invf = 10000.0 ** (-np.arange(64, dtype=np.float64) / 64.0)
    rot = np.zeros((2, 128), np.float64)
    rot[0, :64] = invf / (2 * math.pi)
    rot[0, 64:] = invf / (2 * math.pi)
    rot[1, 64:] = 0.25
    c["c_rot"] = rot.astype(np.float32)
    cv = np.zeros((128, 16), np.float64)
    for h in range(4):
        cv[:, h] = GAM[h] ** (-(l + 1))
        cv[:, 4 + h] = GAM[h] ** (l + 1)
        cv[:, 8 + h] = GAM[h] ** (2 * (l + 1))
    cv[:, 12] = -0.5
    cv[:, 13] = 1.0
    c["c_vec"] = cv.astype(np.float32)
    return c


_NC_CACHE = {}


def _run(inputs, NPRE, NSUP, seq):
    x = np.asarray(inputs["x"], dtype=np.float32)
    pos = np.asarray(inputs["positions"], dtype=np.int32)
    B = x.shape[0]
    M = NT * NSUP * 128
    NV = NPRE + 1 + NT * NSUP
    assert seq == 2 * M and (NPRE + 1) * 128 == M
    key = (NPRE, NSUP)
    if key not in _NC_CACHE:
        _NC_CACHE[key] = build(NPRE, NSUP)
    nc = _NC_CACHE[key]
    consts = _consts()
    shared = {
        "norm1_w": np.ascontiguousarray(inputs["norm1_w"][0], np.float32),
        "norm2_w": np.ascontiguousarray(inputs["norm2_w"][0], np.float32),
        "final_norm_w": np.ascontiguousarray(inputs["final_norm_w"], np.float32).reshape(1, D),
        "w_in": np.ascontiguousarray(inputs["w_in"][0], np.float32),
        "w_out": np.ascontiguousarray(inputs["w_out"][0], np.float32),
        "ffn_w_up": np.ascontiguousarray(inputs["ffn_w_up"][0], np.float32),
        "ffn_w_down": np.ascontiguousarray(inputs["ffn_w_down"][0], np.float32),
        "ret_norm_w": np.ascontiguousarray(inputs["ret_norm_w"][0], np.float32).reshape(1, 512),
        "ret_norm_b": np.ascontiguousarray(inputs["ret_norm_b"][0], np.float32).reshape(1, 512),
        "gla_norm_w": np.ascontiguousarray(inputs["gla_norm_w"][0], np.float32).reshape(1, 512),
        "gla_gate_w2": np.ascontiguousarray(inputs["gla_gate_w2"][0], np.float32),
        "gla_gate_b": np.ascontiguousarray(inputs["gla_gate_b"][0], np.float32).reshape(1, 256),
        "ffn_conv_w": np.ascontiguousarray(inputs["ffn_conv_w"][0], np.float32),
        "ffn_conv_b": np.ascontiguousarray(inputs["ffn_conv_b"][0], np.float32),
    }
    shared.update(consts)
    in_maps = []
    for core in range(2 * B):
        b, half = core // 2, core % 2
        if half == 0:
            xs = np.concatenate([np.zeros((M, D), np.float32), x[b, :M]], axis=0)
            ps = np.concatenate([np.zeros((M,), np.int32), pos[b, :M]], axis=0)
        else:
            xs = x[b]
            ps = pos[b]
        m = dict(shared)
        m["xs"] = np.ascontiguousarray(xs)
        m["pos"] = np.ascontiguousarray(ps.reshape(NV, 128))
        in_maps.append(m)
    res = run_bass_kernel_spmd(nc, in_maps, core_ids=list(range(2 * B)))
    out = np.empty((B, seq, D), np.float32)
    for core in range(2 * B):
        b, half = core // 2, core % 2
        out[b, half * M:(half + 1) * M] = res.results[core]["y"]
    return out


def kernel(**inputs):
    return _run(inputs, NPRE=31, NSUP=8, seq=8192)
```

```python
import math
import os
from contextlib import ExitStack

import numpy as np
import concourse.bass as bass
import concourse.mybir as mybir
from concourse.bass_utils import run_bass_kernel_spmd

F32 = mybir.dt.float32
BF16 = mybir.dt.bfloat16
I32 = mybir.dt.int32
ALU = mybir.AluOpType
AF = mybir.ActivationFunctionType

D = 1024
DIN = 3600
DFF = 2816
NG = 22
EPS = 1e-6
NT = 4
GAM = [1.0 - 2.0 ** (-5.0 - h) for h in range(4)]
C_RQ, C_RK, C_RV, C_RG, C_GQ, C_GK, C_GV, C_GG, C_GL = 0, 512, 1024, 1536, 2048, 2304, 2560, 3072, 3584


class Prog:
    ENG = ("pe", "act", "dve", "pool", "sp")

    def __init__(self, nc, stack):
        self.nc = nc
        self.stack = stack
        self.streams = {e: [] for e in self.ENG}
        self.sem = {e: stack.enter_context(nc.semaphore("s_" + e)) for e in self.ENG if e != "sp"}
        self.cnt = {e: 0 for e in self.ENG}
        self.waited = {e: {} for e in self.ENG}
        self.last_w = {}
        self.readers = {}
        self.dma_sems = {}
        self.alias = {}
        self.regions = []
        self.strict_same = bool(int(os.environ.get("STRICT_SAME", "1")))
        self.rec = None
        self.filler = None
        self.in_fill = False
        self.n_fill = 0
        self.fill_min = float(os.environ.get("FILL_MIN", "0.7"))
        self.fill_frac = float(os.environ.get("FILL_FRAC", "0.7"))
        self.fill_max = int(os.environ.get("FILL_MAX", "16"))
        self.t_w = {}
        self.t_r = {}
        self.t_eng = {e: 0.0 for e in self.ENG}

    def region(self, key, start, end):
        for (k2, s2, e2) in self.regions:
            if start < e2 and s2 < end:
                self.alias.setdefault(key, []).append(k2)
                self.alias.setdefault(k2, []).append(key)
        self.regions.append((key, start, end))

    def new_dma_sem(self, name, total=None):
        s = self.stack.enter_context(self.nc.semaphore(name))
        self.dma_sems[name] = [s, 0, None if total is None else 16 * total]
        return name

    def _deps(self, eng, reads, writes):
        deps = []
        for r in reads:
            for k in [r] + self.alias.get(r, []):
                t = self.last_w.get(k)
                if t is not None:
                    deps.append((t, True))
        for w in writes:
            for k in [w] + self.alias.get(w, []):
                t = self.last_w.get(k)
                if t is not None:
                    deps.append((t, False))
                for t in self.readers.get(k, ()):
                    deps.append((t, False))
        need = {}
        for (semname, val, semobj, teng), is_raw in deps:
            if teng == eng:
                if eng in ("pe", "sp"):
                    continue
                if not is_raw and not self.strict_same:
                    continue
            if self.waited[eng].get(semname, 0) >= val:
                continue
            if need.get(semname, (0, None))[0] < val:
                need[semname] = (val, semobj)
        for semname, (val, semobj) in need.items():
            self.waited[eng][semname] = val
            self.streams[eng].append(("wait", semobj, val))

    def _commit(self, token, reads, writes):
        for w in writes:
            self.last_w[w] = token
            self.readers[w] = []
        for r in reads:
            if r not in writes:
                self.readers.setdefault(r, []).append(token)

    def record(self):
        self.rec = []
        self.cuts = []

    def cut(self):
        if self.rec is not None:
            self.cuts.append(len(self.rec))

    def stop_chunks(self):
        r, cuts = self.rec, self.cuts
        self.rec = None
        out, prev = [], 0
        for c in cuts + [len(r)]:
            out.append(r[prev:c])
            prev = c
        return out

    def stop(self):
        r, self.rec = self.rec, None
        return r

    DEFC = {"pe": 0.27, "act": 0.6, "dve": 0.5, "pool": 1.2, "sp": 0.1}

    def _est_start(self, eng, reads, writes):
        t = self.t_eng[eng]

        def lat(e2):
            if e2 != eng:
                return 0.2
            return 0.0 if eng in ("pe", "sp") else 0.08
        for r in reads:
            for k in [r] + self.alias.get(r, []):
                v = self.t_w.get(k)
                if v is not None:
                    t = max(t, v[0] + lat(v[1]))
        for w in writes:
            for k in [w] + self.alias.get(w, []):
                v = self.t_w.get(k)
                if v is not None:
                    t = max(t, v[0] + lat(v[1]))
                v = self.t_r.get(k)
                if v is not None:
                    t = max(t, v[0] + lat(v[1]))
        return t

    def _est_commit(self, eng, reads, writes, cost, is_dma=False):
        t0 = self._est_start(eng, reads, writes)
        if os.environ.get("KTRACE") and float(os.environ["KTRACE"].split(",")[0]) <= t0 <= float(os.environ["KTRACE"].split(",")[1]):
            if t0 - self.t_eng[eng] > float(os.environ.get("KSTALL", "0")):
                blk = None
                for k in list(reads) + list(writes):
                    for kk in [k] + self.alias.get(k, []):
                        for dct in (self.t_w, self.t_r):
                            v = dct.get(kk)
                            if v is not None and (blk is None or v[0] > blk[0]):
                                blk = (v[0], v[1], kk)
                print("  %8.2f %-4s %5.2f stall=%.2f blocked_by=%s w=%s" % (t0, eng, cost, t0 - self.t_eng[eng], blk, list(writes)[:2]))
        if is_dma:
            self.t_eng[eng] = t0 + 0.1
            t1 = t0 + cost
        else:
            t1 = t0 + cost
            self.t_eng[eng] = t1
        we = "dma" if is_dma else eng
        for w in writes:
            self.t_w[w] = (t1, we)
            self.t_r.pop(w, None)
        for r in reads:
            if r not in writes:
                if r not in self.t_r or self.t_r[r][0] < t1:
                    self.t_r[r] = (t1, we)

    def _rec_info(self, r):
        if r[0] in ("op", "fill"):
            return r[1], r[3], r[4]
        return r[1], r[5], r[6]

    def play(self, recs):
        for r in recs:
            if r[0] == "op":
                self.op(*r[1:])
            elif r[0] == "fill":
                if self.filler is not None:
                    self.in_fill = True
                    for _ in range(r[6]):
                        self.filler()
                    self.in_fill = False
                    self.n_fill += r[6]
            else:
                self.dma(*r[1:])

    def fill(self, n):
        if self.rec is not None:
            self.rec.append(("fill", "pe", None, [], [], 0.0, n))

    def play_stages(self, stages):
        state = [[0, [0] * len(st[0])] if st else None for st in stages]
        while True:
            best = None
            for si, st in enumerate(stages):
                if state[si] is None:
                    continue
                ph, ptrs = state[si]
                while ph < len(st) and all(ptrs[ci] >= len(st[ph][ci]) for ci in range(len(st[ph]))):
                    ph += 1
                    if ph < len(st):
                        ptrs = [0] * len(st[ph])
                if ph >= len(st):
                    state[si] = None
                    continue
                state[si] = [ph, ptrs]
                for ci, ch in enumerate(st[ph]):
                    if ptrs[ci] < len(ch):
                        e, r, w = self._rec_info(ch[ptrs[ci]])
                        t = self._est_start(e, r, w)
                        if best is None or t < best[0]:
                            best = (t, si, ci)
            if best is None:
                break
            _, si, ci = best
            ph, ptrs = state[si]
            self.play([stages[si][ph][ci][ptrs[ci]]])
            ptrs[ci] += 1

    def play_merged(self, a, b):
        ia = ib = 0
        while ia < len(a) or ib < len(b):
            if ib >= len(b):
                pick_a = True
            elif ia >= len(a):
                pick_a = False
            else:
                ea, ra, wa = self._rec_info(a[ia])
                eb, rb, wb = self._rec_info(b[ib])
                pick_a = self._est_start(ea, ra, wa) <= self._est_start(eb, rb, wb)
            if pick_a:
                self.play([a[ia]]); ia += 1
            else:
                self.play([b[ib]]); ib += 1

    def op(self, eng, fn, reads=(), writes=(), c=None):
        if self.rec is not None:
            self.rec.append(("op", eng, fn, list(reads), list(writes), c))
            return
        if eng == "pe" and self.filler is not None and not self.in_fill:
            stall = self._est_start(eng, reads, writes) - self.t_eng["pe"]
            if stall >= self.fill_min:
                n = min(self.fill_max, int(self.fill_frac * stall / 0.22))
                self.in_fill = True
                for _ in range(n):
                    self.filler()
                self.in_fill = False
                self.n_fill += n
        self._est_commit(eng, reads, writes, c if c is not None else self.DEFC[eng])
        self._deps(eng, reads, writes)
        self.cnt[eng] += 1
        token = ("s_" + eng, self.cnt[eng], self.sem[eng], eng)
        self.streams[eng].append(("op", fn, self.sem[eng], 1))
        self._commit(token, reads, writes)

    def dma(self, eng, semname, out, in_, reads=(), writes=(), slow=False):
        if self.rec is not None:
            self.rec.append(("dma", eng, semname, out, in_, list(reads), list(writes), slow))
            return
        self._est_commit(eng, reads, writes, 2.5, is_dma=True)
        self._deps(eng, reads, writes)
        ent = self.dma_sems[semname]
        ent[1] += 16
        token = (semname, ent[2] if ent[2] is not None else ent[1], ent[0], "dma")
        self.streams[eng].append(("op", lambda e, o=out, i=in_, s=slow: e.dma_start(o, i, allow_slow_non_contiguous=s), ent[0], 16))
        self._commit(token, reads, writes)

    def emit(self):
        nc = self.nc
        for name, ent in self.dma_sems.items():
            assert ent[2] is None or ent[2] == ent[1], (name, ent[1], ent[2])
        engmap = {"pe": "tensor", "act": "scalar", "dve": "vector", "pool": "gpsimd", "sp": "sync"}
        with nc.Block() as block:
            for e in self.ENG:
                items = self.streams[e]

                def body(engine, items=items):
                    for it in items:
                        if it[0] == "wait":
                            engine.wait_ge(it[1], it[2])
                        else:
                            it[1](engine).then_inc(it[2], it[3])
                getattr(block, engmap[e])(body)


def interleave(a, b):
    out = []
    na, nb = len(a), len(b)
    ia = ib = 0
    while ia < na or ib < nb:
        if ib >= nb or (ia < na and ia * nb <= ib * na):
            out.append(a[ia]); ia += 1
        else:
            out.append(b[ib]); ib += 1
    return out


def build(NPRE, NSUP):
    NV = NPRE + 1 + NT * NSUP
    NMAIN = NT * NSUP
    nc = bass.Bass("TRN2", target_bir_lowering=False)
    din = lambda name, shape, dt=F32: nc.dram_tensor(name, shape, dt, kind="ExternalInput").ap()
    xs_d = din("xs", [NV * 128, D])
    pos_d = din("pos", [NV, 128], I32)
    n1w_d = din("norm1_w", [D]); n2w_d = din("norm2_w", [D]); fnw_d = din("final_norm_w", [1, D])
    win_d = din("w_in", [D, DIN]); wout_d = din("w_out", [D, D])
    wup_d = din("ffn_w_up", [D, 2 * DFF]); wdn_d = din("ffn_w_down", [DFF, D])
    rnw_d = din("ret_norm_w", [1, 512]); rnb_d = din("ret_norm_b", [1, 512]); gnw_d = din("gla_norm_w", [1, 512])
    gw2_d = din("gla_gate_w2", [16, 256]); gb_d = din("gla_gate_b", [1, 256])
    cw_d = din("ffn_conv_w", [3, 2 * DFF]); cb_d = din("ffn_conv_b", [2 * DFF])
    ident_d = din("c_ident", [128, 128]); maskr_d = din("c_maskr", [128, 128]); maskg_d = din("c_maskg", [128, 128])
    triu_d = din("c_triu", [128, 128]); tril_d = din("c_tril", [128, 128])
    rot_d = din("c_rot", [2, 128]); cvec_d = din("c_vec", [128, 16])
    y_d = nc.dram_tensor("y", [NMAIN * 128, D], F32, kind="ExternalOutput").ap()
    wup_s = nc.dram_tensor("wup_s", [D, 2 * DFF], BF16, kind="Internal").ap()
    wdn_s = nc.dram_tensor("wdn_s", [DFF, D], BF16, kind="Internal").ap()

    with ExitStack() as st:
        P = Prog(nc, st)
        sb_bytes = [0]

        def T(name, shape, dt):
            n = 1
            for d_ in shape[1:]:
                n *= d_
            sb_bytes[0] += n * (2 if dt == BF16 else 4)
            return st.enter_context(nc.sbuf_tensor(name, shape, dt))
        bank = [st.enter_context(nc.psum_tensor("bank%d" % i, [128, 512], F32)) for i in range(8)]
        bkey = lambda i: "bank%d" % i
        bf = lambda i: bank[i][:].bitcast(BF16).rearrange("p (k c) -> p k c", c=128)
        f3 = lambda i: bank[i][:].rearrange("p (h c) -> p h c", c=128)

        win = T("win", [128, 8, DIN], BF16)
        wout = T("wout", [128, 8, D], BF16)
        NWU, NWD = 2, 2
        wu = [T("wu%d" % i, [128, 2, 8, 256], BF16) for i in range(NWU)]
        fnw = T("fnw", [128, D], F32)
        rnw = T("rnw", [128, 512], F32); rnb = T("rnb", [128, 512], F32); gnw = T("gnw", [128, 512], F32)
        n1wT = T("n1wT", [128, 8], F32); n2wT = T("n2wT", [128, 8], F32)
        identf = T("identf", [128, 128], F32); ident = T("ident", [128, 128], BF16)
        maskr = T("maskr", [128, 128], F32); maskg = T("maskg", [128, 128], F32)
        triu = T("triu", [128, 128], F32); tril = T("tril", [128, 128], F32)
        rotc = T("rotc", [128, 2, 128], F32)
        cvec = T("cvec", [128, 16], F32)
        gw2f = T("gw2f", [17, 256], F32); gw2b = T("gw2b", [17, 256], BF16)
        cwt = T("cwt", [128, 4, 2 * NG], F32)
        posi = T("posi", [NV, 128], I32); posr = T("posr", [NV, 128], F32); posf = T("posf", [128, NV], F32)
        xb = [T("xb%d" % i, [128, D], F32) for i in range(3)]
        hb = [T("hb%d" % i, [128, D], F32) for i in range(NT)]
        hnT = T("hnT", [128, 8, NT * 128], BF16)
        Sr = T("Sr", [128, 4, 128], F32); Srb = T("Srb", [128, 4, 128], BF16)
        Sg = T("Sg", [64, 4, 128], F32); Sgb = T("Sgb", [64, 4, 128], BF16)
        carry = T("carry", [128, 2 * NG, 2], F32)
        sm = T("sm", [128, 64], F32)
        glT = T("glT", [17, 128], BF16)

        KEEP_BYTES = 2 * (4 * 1024 + 2 * 2048 + 3 * 512)
        UNI_BYTES = KEEP_BYTES + 35328
        uni = T("uni", [128, UNI_BYTES // 4], F32)
        off = {"keep": 0, "mix": KEEP_BYTES, "ffn": KEEP_BYTES}

        def carve(kind, key, ncols, dt):
            nbytes = ncols * (2 if dt == BF16 else 4)
            s = off[kind]
            off[kind] += nbytes
            assert off[kind] <= (KEEP_BYTES if kind == "keep" else UNI_BYTES), (kind, key, off[kind])
            P.region(key, s, s + nbytes)
            coff[key] = s
            v = uni[:, s // 4:(s + nbytes) // 4]
            return v.bitcast(BF16) if dt == BF16 else v

        coff = {}

        def carve_at(key, s, ncols, dt):
            nbytes = ncols * (2 if dt == BF16 else 4)
            P.region(key, s, s + nbytes)
            v = uni[:, s // 4:(s + nbytes) // 4]
            return v.bitcast(BF16) if dt == BF16 else v

        AO = []
        for S in range(2):
            d = {}
            for nm in ("qr", "kr", "vr", "vg"):
                d[nm] = carve("keep", "%s_%d" % (nm, S), 512, BF16)
            for nm in ("sgr", "sgg"):
                d[nm] = carve("keep", "%s_%d" % (nm, S), 512, F32)
            for nm in ("qg", "kg", "kh"):
                d[nm] = carve("keep", "%s_%d" % (nm, S), 256, BF16)
            AO.append(d)
        junk = carve("mix", "junk", D, BF16)
        xn = carve("mix", "xn", D, BF16)
        xT_main = carve("mix", "xT", D, BF16).rearrange("p (k c) -> p k c", c=128)
        rot = [carve("mix", "rot%d" % i, 256, F32).rearrange("p (h c) -> p h c", c=64) for i in range(4)]
        glb = carve("mix", "glb", 16, BF16)
        e1 = carve("mix", "e1", 256, F32); nls = carve("mix", "nls", 256, F32)
        eq = carve("mix", "eq", 256, F32); ek = carve("mix", "ek", 256, F32); erc_main = carve("mix", "erc", 256, F32)
        qkT = carve("mix", "qkT", 1024, BF16).rearrange("p (k c) -> p k c", c=128)
        scm = carve("mix", "scm", 512, BF16).rearrange("p (h c) -> p h c", c=128)
        dsc = carve("mix", "dsc", 512, F32).rearrange("p (h c) -> p h c", c=128)
        qkgT = carve("mix", "qkgT", 1024, BF16).rearrange("p (k c) -> p k c", c=128)
        nr = carve("mix", "nr", 512, F32); ng = carve("mix", "ng", 512, F32)
        mixed = carve("mix", "mixed", D, BF16)
        mT = carve("mix", "mT", D, BF16).rearrange("p (k c) -> p k c", c=128)
        hn = carve("mix", "hn", D, BF16)
        cs_main = carve("mix", "cs", 128, F32); ckf = carve("mix", "ckf", 128, F32); ckm = carve("mix", "ckm", 128, F32)
        cki = T("cki_t", [128, 128], I32)[:]
        xT_alt1 = carve_at("xT_alt1", coff["mixed"], D, BF16).rearrange("p (k c) -> p k c", c=128)
        xT_alt2 = carve_at("xT_alt2", coff["mT"], D, BF16).rearrange("p (k c) -> p k c", c=128)
        erc_alt = carve_at("erc_alt", coff["nr"], 256, F32)
        cs_alt1 = carve_at("cs_alt1", coff["nr"] + 1024, 128, F32)
        cs_alt2 = carve_at("cs_alt2", coff["nr"] + 1536, 128, F32)
        NACC = 6
        acc = [carve("ffn", "acc%d" % i, NT * 128, F32) for i in range(NACC)]
        actT = [carve("ffn", "actT%d" % g, NT * 128, BF16) for g in range(NG)]
        dumr = T("dumr", [128, 512], BF16)
        corr = T("corr", [128, 2 * NG, 2], F32)
        ctmp = T("ctmp", [128, 2 * NG], F32)

        if os.environ.get("KDEBUG"):
            print("SBUF bytes/partition:", sb_bytes[0])
        for i in range(3):
            P.new_dma_sem("xld%d" % i)
        P.new_dma_sem("su", total=10)
        P.new_dma_sem("su2", total=11)
        for i in range(12):
            P.new_dma_sem("stg%d" % i)
        P.new_dma_sem("sucB", total=26)
        P.new_dma_sem("sucf", total=38)
        su = lambda out, in_, key, slow=False: P.dma("sp", "su", out, in_, writes=[key], slow=slow)
        su2 = lambda out, in_, key, slow=False: P.dma("sp", "su2", out, in_, writes=[key], slow=slow)
        su(identf[:], ident_d, "identf"); su(cvec[:], cvec_d, "cvec")
        su(rotc[:, 0, :], rot_d[0:1, :].partition_broadcast(128), "rotc0"); su(rotc[:, 1, :], rot_d[1:2, :].partition_broadcast(128), "rotc1")
        su(posi[:], pos_d, "posi")
        su(n1wT[:], n1w_d.rearrange("(k p) -> p k", p=128), "n1wT", slow=True)
        su(gw2f[0:16, :], gw2_d, "gw2fa"); su(gw2f[16:17, :], gb_d, "gw2fb")
        su(triu[:], triu_d, "triu"); su(tril[:], tril_d, "tril")
        P.dma("sp", "xld0", xb[0][:], xs_d[0:128, :], writes=["xb0"])
        su2(maskr[:], maskr_d, "maskr"); su2(maskg[:], maskg_d, "maskg")
        su2(fnw[:], fnw_d.partition_broadcast(128), "fnw")
        su2(rnw[:], rnw_d.partition_broadcast(128), "rnw"); su2(rnb[:], rnb_d.partition_broadcast(128), "rnb"); su2(gnw[:], gnw_d.partition_broadcast(128), "gnw")
        su2(n2wT[:], n2w_d.rearrange("(k p) -> p k", p=128), "n2wT", slow=True)
        for i in range(3):
            su2(cwt[:, i, :], cw_d[i].rearrange("(g p) -> p g", p=128), "cwt%d" % i, slow=True)
        su2(cwt[:, 3, :], cb_d.rearrange("(g p) -> p g", p=128), "cwt3", slow=True)
        win_v = win_d.rearrange("(k p) c -> p k c", p=128)
        wout_v = wout_d.rearrange("(k p) c -> p k c", p=128)
        RA = [(C_RK, C_RG), (C_GK, C_GG), (C_GL, DIN)]
        RB = [(C_RQ, C_RK), (C_RG, C_GK), (C_GG, C_GL)]
        WINA = ["winA%d_%d" % (k, i) for k in range(8) for i in range(3)]
        WINB = ["winB%d_%d" % (k, i) for k in range(8) for i in range(3)]
        WIN = WINA + WINB
        WOUT = ["wout0", "wout1"]
        WUPS = ["wup_s%d" % k for k in range(16)]
        WDNS = ["wdn_s%d" % k for k in range(22)]
        stage = [uni[:, KEEP_BYTES // 4 + sl * 1808:KEEP_BYTES // 4 + (sl + 1) * 1808] for sl in range(4)]
        for sl in range(4):
            P.region("stage%d" % sl, KEEP_BYTES + sl * 1808 * 4, KEEP_BYTES + (sl + 1) * 1808 * 4)
        soff = [0, 1024, 1792]
        for sl in range(4):
            for i, (c0, c1) in enumerate([(C_RK, C_RG), (C_GK, C_GG), (C_GL, DIN)]):
                b0 = KEEP_BYTES + sl * 1808 * 4 + soff[i] * 4
                P.region("stage%d_%d" % (sl, i), b0, b0 + (c1 - c0) * 4)
        for k in range(8):
            sl = k % 4
            for i, (c0, c1) in enumerate(RA):
                n_ = c1 - c0
                P.dma("sp", "stg%d" % (sl * 3 + i), stage[sl][:, soff[i]:soff[i] + n_], win_v[:, k, c0:c1], writes=["stage%d_%d" % (sl, i)])
            for i, (c0, c1) in enumerate(RA):
                n_ = c1 - c0
                eng_ = ("dve", "act", "pool")[i]
                if eng_ == "act":
                    P.op("act", lambda e, k=k, sl=sl, i=i, c0=c0, c1=c1, n_=n_: e.copy(win[:, k, c0:c1], stage[sl][:, soff[i]:soff[i] + n_]),
                         ["stage%d_%d" % (sl, i)], ["winA%d_%d" % (k, i)], 0.9)
                else:
                    P.op(eng_, lambda e, k=k, sl=sl, i=i, c0=c0, c1=c1, n_=n_: e.tensor_copy(win[:, k, c0:c1], stage[sl][:, soff[i]:soff[i] + n_]),
                         ["stage%d_%d" % (sl, i)], ["winA%d_%d" % (k, i)], 1.0)
        bulk = []
        for k in range(16):
            bulk.append(("sucf", wup_s[64 * k:64 * k + 64, :], wup_d[64 * k:64 * k + 64, :], WUPS[k]))
        for k in range(8):
            for i, (c0, c1) in enumerate(RB):
                bulk.append(("sucB", win[:, k, c0:c1], win_v[:, k, c0:c1], "winB%d_%d" % (k, i)))
        for k in range(2):
            bulk.append(("sucB", wout[:, 4 * k:4 * k + 4, :], wout_v[:, 4 * k:4 * k + 4, :], WOUT[k]))
        for k in range(22):
            bulk.append(("sucf", wdn_s[128 * k:128 * k + 128, :], wdn_d[128 * k:128 * k + 128, :], WDNS[k]))

        def issue_bulk(n):
            for _ in range(min(n, len(bulk))):
                sem_, o_, i_, key_ = bulk.pop(0)
                P.dma("pool", sem_, o_, i_, writes=[key_])
        wup_sv = wup_s.rearrange("(k p) c -> p k c", p=128)
        wdn_sv = wdn_s.rearrange("(k p) c -> p k c", p=128)

        dve = lambda fn, r, w, c=None: P.op("dve", fn, r, w, c)
        act = lambda fn, r, w, c=None: P.op("act", fn, r, w, c)
        pool = lambda fn, r, w, c=None: P.op("pool", fn, r, w, c)
        pe = lambda fn, r, w, c=None: P.op("pe", fn, r, w, c)

        dve(lambda e: e.tensor_copy(ident[:], identf[:]), ["identf"], ["ident"])
        dve(lambda e: e.tensor_copy(gw2b[:], gw2f[:]), ["gw2fa", "gw2fb"], ["gw2b"])
        dve(lambda e: e.tensor_copy(posr[:], posi[:]), ["posi"], ["posr"])
        pe(lambda e: e.matmul(bank[3][:, 0:NV], posr[:], identf[0:NV, 0:NV], start=True, stop=True), ["posr", "identf"], [bkey(3)])
        dve(lambda e: e.tensor_copy(posf[:], bank[3][:, 0:NV]), [bkey(3)], ["posf"])
        dve(lambda e: e.memset(Sr[:], 0.0), [], ["Sr"])
        dve(lambda e: e.memset(Sg[:], 0.0), [], ["Sg"])
        dve(lambda e: e.memset(Srb[:], 0.0), [], ["Srb"])
        dve(lambda e: e.memset(Sgb[:], 0.0), [], ["Sgb"])
        dve(lambda e: e.memset(carry[:], 0.0), [], ["carry%d" % c for c in range(2 * NG)])
        dve(lambda e: e.memset(glT[:], 1.0), [], ["glT"])
        dve(lambda e: e.memset(dumr[:], 0.5), [], ["dumr"])
        fill_i = [0]

        def filler():
            b_ = (5, 7)[fill_i[0] % 2]
            fill_i[0] += 1
            P.op("pe", lambda e, b_=b_: e.matmul(bank[b_][:], ident[:], dumr[:], start=True, stop=True), ["ident", "dumr"], ["fillbank%d" % b_], 0.22)

        for i in range(NT):
            P.new_dma_sem("yst%d" % i)
        for i in range(NWU):
            P.new_dma_sem("wul%d_0" % i)
            P.new_dma_sem("wul%d_1" % i)

        def load_x(t):
            P.dma("sp", "xld%d" % (t % 3), xb[t % 3][:], xs_d[t * 128:(t + 1) * 128, :], writes=["xb%d" % (t % 3)])

        def rms_and_transpose(src, srckey, ss_col, nwT, nwkey, dstT, dstkeys, xn_t, xnkey, tb):
            ss = sm[:, ss_col:ss_col + 1]
            rs = sm[:, ss_col + 1:ss_col + 2]
            act(lambda e: e.activation(junk, src, AF.Square, accum_out=ss), [srckey], ["ss%d" % ss_col, "junk"], 1.1)
            pool(lambda e: e.tensor_scalar(rs, ss, 1.0 / D, EPS, ALU.mult, ALU.add), ["ss%d" % ss_col], ["rs%d" % ss_col], 0.2)
            pool(lambda e: e.tensor_tensor(rs, rs, cvec[:, 12:13], ALU.pow), ["rs%d" % ss_col, "cvec"], ["rs%d" % ss_col], 0.5)
            dve(lambda e: e.tensor_scalar(xn_t, src, rs, None, ALU.mult), [srckey, "rs%d" % ss_col], [xnkey], 0.8)
            for k in range(8):
                pe(lambda e, k=k: e.transpose(bf(tb)[:, k, :], xn_t[:, k * 128:(k + 1) * 128], ident[:]), [xnkey, "ident"], [bkey(tb)], 0.1)
            dve(lambda e: e.tensor_tensor(dstT, bf(tb), nwT[:].unsqueeze(2).to_broadcast([128, 8, 128]), ALU.mult), [bkey(tb), nwkey], dstkeys, 1.25)

        pj_i = [0]

        def proj_block(lhsT_t, lhskey, w_t, wkey, c0, ncol, banks=(1, 2)):
            b = banks[pj_i[0] % 2]
            pj_i[0] += 1
            for k in range(8):
                pe(lambda e, k=k, b=b: e.matmul(bank[b][:, 0:ncol], lhsT_t[:, k, :], w_t[:, k, c0:c0 + ncol], start=(k == 0), stop=(k == 7)),
                   [lhskey] + wkey, [bkey(b)], 0.27 if ncol > 256 else (0.16 if ncol > 16 else 0.07))
            return b

        def rotary(b, dst, dstkey, cs, CS):
            ps = bank[b][:].rearrange("p (h t c) -> p h t c", h=4, t=2)
            cosb = cs[:, 64:128].unsqueeze(1).to_broadcast([128, 4, 64])
            sinb = cs[:, 0:64].unsqueeze(1).to_broadcast([128, 4, 64])
            d4 = dst.rearrange("p (h t c) -> p h t c", h=4, t=2)
            dve(lambda e: e.tensor_tensor(rot[0], ps[:, :, 0, :], cosb, ALU.mult), [bkey(b), CS], ["rot0"], 0.37)
            dve(lambda e: e.tensor_tensor(rot[1], ps[:, :, 1, :], sinb, ALU.mult), [bkey(b), CS], ["rot1"], 0.37)
            dve(lambda e: e.tensor_tensor(rot[2], ps[:, :, 1, :], cosb, ALU.mult), [bkey(b), CS], ["rot2"], 0.37)
            dve(lambda e: e.tensor_tensor(rot[3], ps[:, :, 0, :], sinb, ALU.mult), [bkey(b), CS], ["rot3"], 0.37)
            pool(lambda e: e.tensor_tensor(d4[:, :, 0, :], rot[0], rot[1], ALU.subtract), ["rot0", "rot1"], [dstkey + "a"], 0.73)
            pool(lambda e: e.tensor_tensor(d4[:, :, 1, :], rot[2], rot[3], ALU.add), ["rot2", "rot3"], [dstkey + "b"], 0.73)

        def stage_a(t, full):
            S = t % 2
            o = AO[S]
            K_ = lambda nm: "%s_%d" % (nm, S)
            xk = "xb%d" % (t % 3)
            x_t = xb[t % 3]
            E3 = t % 3
            edec = sm[0:64, 8 + 4 * E3:12 + 4 * E3] if E3 < 2 else sm[0:64, 60:64]
            KE = "edec_%d" % E3
            q3 = t % 3 if not full else 0
            xT, XT = [(xT_main, "xT"), (xT_alt1, "xT_alt1"), (xT_alt2, "xT_alt2")][q3]
            cs, CS = [(cs_main, "cs"), (cs_alt1, "cs_alt1"), (cs_alt2, "cs_alt2")][q3]
            use_alt = (not full) and (t % 2 == 0)
            erc, ERC = (erc_alt, "erc_alt") if use_alt else (erc_main, "erc")
            pj_i[0] = 0
            WIN = (WINA + WINB) if full else WINA
            dve(lambda e: e.scalar_tensor_tensor(cs, rotc[:, 0, :], posf[:, t:t + 1], rotc[:, 1, :], ALU.mult, ALU.add), ["rotc0", "rotc1", "posf"], [CS], 0.27)
            dve(lambda e: e.tensor_copy(cki, cs), [CS], ["cki"], 0.27)
            dve(lambda e: e.tensor_copy(ckf, cki), ["cki"], ["ckf"], 0.27)
            dve(lambda e: e.tensor_tensor(cs, cs, ckf, ALU.subtract), [CS, "ckf"], [CS], 0.27)
            dve(lambda e: e.tensor_scalar(ckm, cs, 0.5, None, ALU.is_gt), [CS], ["ckm"], 0.27)
            dve(lambda e: e.tensor_tensor(cs, cs, ckm, ALU.subtract), [CS, "ckm"], [CS], 0.27)
            act(lambda e: e.activation(cs, cs, AF.Sin, scale=2 * math.pi), [CS], [CS], 0.3 if full else 1.6)

            rms_and_transpose(x_t[:], xk, 0, n1wT, "n1wT", xT, [XT], xn, "xn", 0)
            P.cut()

            b = 3
            for k in range(8):
                pe(lambda e, k=k: e.matmul(bank[3][:, 128:144], xT[:, k, :], win[:, k, C_GL:C_GL + 16], start=(k == 0), stop=(k == 7)), [XT] + WIN, [bkey(3)], 0.07)
            dve(lambda e: e.tensor_copy(glb, bank[3][:, 128:144]), [bkey(3)], ["glb"], 0.2)
            pe(lambda e: e.transpose(bank[3][:].bitcast(BF16)[0:16, 0:128], glb, ident[:]), ["glb", "ident"], [bkey(3)], 0.1)
            dve(lambda e: e.tensor_copy(glT[0:16, :], bank[3][:].bitcast(BF16)[0:16, 0:128]), [bkey(3)], ["glT"], 0.2)
            pe(lambda e: e.matmul(bank[3][:, 256:512], glT[:], gw2b[:], start=True, stop=True), ["glT", "gw2b"], [bkey(3)], 0.3)
            act(lambda e: e.activation(e1, bank[3][:, 256:512], AF.Exp, scale=-1.0), [bkey(3)], ["e1"], 1.7)
            act(lambda e: e.activation(nls, e1, AF.Ln, bias=1.0), ["e1"], ["nls"], 0.42)
            if full:
                pe(lambda e: e.matmul(bank[0][:, 0:256], triu[:], nls, start=True, stop=True), ["triu", "nls"], [bkey(0)], 0.55)
            cb = 0 if full else 5
            if not full:
                P.fill(int(os.environ.get("FILL_RC", "12")))
            pe(lambda e: e.matmul(bank[cb][:, 256:512], tril[:], nls, start=True, stop=True), ["tril", "nls"], [bkey(cb)], 0.55)
            for h in range(4):
                pe(lambda e, h=h: e.matmul(bank[3][0:64, 64 + h:65 + h], nls[:, h * 64:(h + 1) * 64], cvec[:, 13:14], start=True, stop=True),
                   ["nls", "cvec"], [bkey(3)], 0.1)
            if full:
                act(lambda e: e.activation(eq, bank[0][:, 0:256], AF.Exp, scale=-1.0 / 16), [bkey(0)], ["eq"], 0.35)
                act(lambda e: e.activation(ek, bank[0][:, 0:256], AF.Exp, scale=1.0 / 16), [bkey(0)], ["ek"], 0.3)
            act(lambda e: e.activation(erc, bank[cb][:, 256:512], AF.Exp, scale=-1.0 / 16), [bkey(cb)], [ERC], 0.3)
            act(lambda e: e.activation(edec, bank[3][0:64, 64:68], AF.Exp, scale=-1.0 / 16), [bkey(3)], [KE], 0.2)

            if full:
                b = proj_block(xT, XT, win, WIN, C_GQ, 512, banks=(0, 0))
                dve(lambda e, b=b: e.scalar_tensor_tensor(o["qg"], bank[b][:, 0:256], 0.125, eq, ALU.mult, ALU.mult), [bkey(b), "eq"], [K_("qg")], 0.42)
                dve(lambda e, b=b: e.tensor_tensor(o["kg"], bank[b][:, 256:512], ek, ALU.mult), [bkey(b), "ek"], [K_("kg")], 0.35)
                dve(lambda e, b=b: e.tensor_tensor(o["kh"], bank[b][:, 256:512], erc, ALU.mult), [bkey(b), ERC], [K_("kh")], 0.35)
            P.cut()
            pj_i[0] = 0
            if not full:
                b = proj_block(xT, XT, win, WIN, C_GK, 256)
                dve(lambda e, b=b: e.tensor_tensor(o["kh"], bank[b][:, 0:256], erc, ALU.mult), [bkey(b), ERC], [K_("kh")], 0.35)
            if full:
                b = proj_block(xT, XT, win, WIN, C_RQ, 512)
                rotary(b, o["qr"], K_("qr"), cs, CS)
            b = proj_block(xT, XT, win, WIN, C_RV, 512)
            for h in range(4):
                act(lambda e, h=h, b=b: e.activation(o["vr"][:, h * 128:(h + 1) * 128], bank[b][:, h * 128:(h + 1) * 128], AF.Identity, scale=cvec[:, h:h + 1]),
                    [bkey(b), "cvec"], [K_("vr")], 0.37)
            b = proj_block(xT, XT, win, WIN, C_RK, 512)
            rotary(b, o["kr"], K_("kr"), cs, CS)
            b = proj_block(xT, XT, win, WIN, C_GV, 512)
            act(lambda e, b=b: e.copy(o["vg"], bank[b][:]), [bkey(b)], [K_("vg")])
            if full:
                b = proj_block(xT, XT, win, WIN, C_RG, 512)
                act(lambda e, b=b: e.activation(o["sgr"], bank[b][:], AF.Silu), [bkey(b)], [K_("sgr")], 1.9)
                b = proj_block(xT, XT, win, WIN, C_GG, 512)
                act(lambda e, b=b: e.activation(o["sgg"], bank[b][:], AF.Silu), [bkey(b)], [K_("sgg")])

        def stage_b(t, full, slot):
            S = t % 2
            o = AO[S]
            K_ = lambda nm: "%s_%d" % (nm, S)
            xk = "xb%d" % (t % 3)
            x_t = xb[t % 3]
            E3 = t % 3
            edec = sm[0:64, 8 + 4 * E3:12 + 4 * E3] if E3 < 2 else sm[0:64, 60:64]
            KE = "edec_%d" % E3
            qr, kr, vr, vg, qg, kg, kh, sgr, sgg = (o[n] for n in ("qr", "kr", "vr", "vg", "qg", "kg", "kh", "sgr", "sgg"))
            KR = [K_("kr") + "a", K_("kr") + "b"]
            QR = [K_("qr") + "a", K_("qr") + "b"]
            if full:
                pool(lambda e: e.tensor_tensor(ng, rnb[:], sgr, ALU.mult), ["rnb", K_("sgr")], ["ng"], 1.26)
                pool(lambda e: e.tensor_tensor(sgr, rnw[:], sgr, ALU.mult), ["rnw", K_("sgr")], [K_("sgr")], 1.26)
                pool(lambda e: e.tensor_tensor(sgg, gnw[:], sgg, ALU.mult), ["gnw", K_("sgg")], [K_("sgg")], 1.26)
            if full:
                for h in range(4):
                    pe(lambda e, h=h: e.transpose(bf(5)[:, h, :], qr[:, h * 128:(h + 1) * 128], ident[:]), QR + ["ident"], [bkey(5)], 0.1)
                    pe(lambda e, h=h: e.transpose(bf(5)[:, 4 + h, :], kr[:, h * 128:(h + 1) * 128], ident[:]), KR + ["ident"], [bkey(5)], 0.1)
                act(lambda e: e.copy(qkT, bf(5)), [bkey(5)], ["qkT"], 0.9)
                for h in range(4):
                    pe(lambda e, h=h: e.matmul(f3(6)[:, h, :], qkT[:, 4 + h, :], qkT[:, h, :], start=True, stop=True), ["qkT"], [bkey(6)], 0.1)
                dve(lambda e: e.tensor_tensor(scm, f3(6), maskr[:].unsqueeze(1).to_broadcast([128, 4, 128]), ALU.mult), [bkey(6), "maskr"], ["scm"], 0.7)
                for h in range(4):
                    pe(lambda e, h=h: e.matmul(f3(7)[:, h, :], scm[:, h, :], vr[:, h * 128:(h + 1) * 128], start=True, stop=False), ["scm", K_("vr")], [bkey(7)], 0.1)
                    pe(lambda e, h=h: e.matmul(f3(7)[:, h, :], qkT[:, h, :], Srb[:, h, :], start=False, stop=True), ["qkT", "Srb"], [bkey(7)], 0.1)
            for h in range(4):
                pe(lambda e, h=h: e.matmul(f3(4)[:, h, :], kr[:, h * 128:(h + 1) * 128], vr[:, h * 128:(h + 1) * 128], start=True, stop=True),
                   KR + [K_("vr")], [bkey(4)], 0.1)
            for h in range(4):
                gc = GAM[h] ** 128
                act(lambda e, h=h, gc=gc: e.activation(dsc[:, h, :], f3(4)[:, h, :], AF.Identity, scale=gc / math.sqrt(128.0)), [bkey(4)], ["dsc"], 0.3)
            for h in range(4):
                gc = GAM[h] ** 128
                dve(lambda e, h=h, gc=gc: e.scalar_tensor_tensor(Sr[:, h, :], Sr[:, h, :], gc, dsc[:, h, :], ALU.mult, ALU.add), ["Sr", "dsc"], ["Sr"], 0.3)
            act(lambda e: e.copy(Srb[:], Sr[:]), ["Sr"], ["Srb"], 0.6)
            P.cut()

            if full:
                for h in range(4):
                    pe(lambda e, h=h: e.transpose(bf(5)[0:64, h, :], qg[:, h * 64:(h + 1) * 64], ident[:]), [K_("qg"), "ident"], [bkey(5)], 0.1)
                    pe(lambda e, h=h: e.transpose(bf(5)[0:64, 4 + h, :], kg[:, h * 64:(h + 1) * 64], ident[:]), [K_("kg"), "ident"], [bkey(5)], 0.1)
                act(lambda e: e.copy(qkgT[0:64], bf(5)[0:64]), [bkey(5)], ["qkgT"], 0.9)
                for h in range(4):
                    pe(lambda e, h=h: e.matmul(f3(6)[:, h, :], qkgT[0:64, 4 + h, :], qkgT[0:64, h, :], start=True, stop=True), ["qkgT"], [bkey(6)], 0.1)
                dve(lambda e: e.tensor_tensor(scm, f3(6), maskg[:].unsqueeze(1).to_broadcast([128, 4, 128]), ALU.mult), [bkey(6), "maskg"], ["scm"], 0.7)
                for h in range(4):
                    pe(lambda e, h=h: e.matmul(f3(4)[:, h, :], scm[:, h, :], vg[:, h * 128:(h + 1) * 128], start=True, stop=False), ["scm", K_("vg")], [bkey(4)], 0.1)
                    pe(lambda e, h=h: e.matmul(f3(4)[:, h, :], qkgT[0:64, h, :], Sgb[:, h, :], start=False, stop=True), ["qkgT", "Sgb"], [bkey(4)], 0.1)
            for h in range(4):
                pe(lambda e, h=h: e.matmul(f3(6)[0:64, h, :], kh[:, h * 64:(h + 1) * 64], vg[:, h * 128:(h + 1) * 128], start=True, stop=True), [K_("kh"), K_("vg")], [bkey(6)], 0.1)
            for h in range(4):
                dve(lambda e, h=h: e.scalar_tensor_tensor(Sg[:, h, :], Sg[:, h, :], edec[:, h:h + 1], f3(6)[0:64, h, :], ALU.mult, ALU.add),
                    ["Sg", KE, bkey(6)], ["Sg"], 0.3)
            act(lambda e: e.copy(Sgb[:], Sg[:]), ["Sg"], ["Sgb"], 0.45)
            P.cut()
            if not full:
                return

            st6 = sm[:, 16:40].rearrange("p (h c) -> p h c", c=6)
            mv = sm[:, 40:48].rearrange("p (h c) -> p h c", c=2)
            for h in range(4):
                dve(lambda e, h=h: e.bn_stats(st6[:, h, :], f3(7)[:, h, :]), [bkey(7)], ["st6"], 0.3)
            for h in range(4):
                dve(lambda e, h=h: e.bn_aggr(mv[:, h, :], st6[:, h, :]), ["st6"], ["mv"], 0.18)
            rsr = sm[:, 48:52]
            dve(lambda e: e.tensor_tensor(rsr, mv[:, :, 1], cvec[:, 8:12], ALU.mult), ["mv", "cvec"], ["rsr"], 0.17)
            dve(lambda e: e.tensor_scalar(rsr, rsr, EPS, None, ALU.add), ["rsr"], ["rsr"], 0.17)
            pool(lambda e: e.tensor_tensor(rsr, rsr, cvec[:, 12:13].to_broadcast([128, 4]), ALU.pow), ["rsr", "cvec"], ["rsr"], 1.0)
            dve(lambda e: e.tensor_tensor(rsr, rsr, cvec[:, 4:8], ALU.mult), ["rsr", "cvec"], ["rsr"], 0.17)
            for h in range(4):
                dve(lambda e, h=h: e.tensor_scalar(nr[:, h * 128:(h + 1) * 128], f3(7)[:, h, :], mv[:, h, 0:1], rsr[:, h:h + 1], ALU.subtract, ALU.mult),
                    [bkey(7), "mv", "rsr"], ["nr"], 0.41)
            dve(lambda e: e.tensor_tensor(nr, nr, sgr, ALU.mult), ["nr", K_("sgr")], ["nr"], 0.69)
            dve(lambda e: e.tensor_tensor(mixed[:, 0:512], nr, ng, ALU.add), ["nr", "ng"], ["mixeda"], 0.69)
            P.cut()
            ssg = sm[:, 52:56]
            rsg = sm[:, 56:60]
            for h in range(4):
                act(lambda e, h=h: e.activation(junk[:, h * 128:(h + 1) * 128], f3(4)[:, h, :], AF.Square, accum_out=ssg[:, h:h + 1]), [bkey(4)], ["ssg", "junk"], 0.37)
            pool(lambda e: e.tensor_scalar(rsg, ssg, 1.0 / 128, EPS, ALU.mult, ALU.add), ["ssg"], ["rsg"], 0.2)
            pool(lambda e: e.tensor_tensor(rsg, rsg, cvec[:, 12:13].to_broadcast([128, 4]), ALU.pow), ["rsg", "cvec"], ["rsg"], 1.0)
            for h in range(4):
                dve(lambda e, h=h: e.scalar_tensor_tensor(mixed[:, 512 + h * 128:512 + (h + 1) * 128], f3(4)[:, h, :], rsg[:, h:h + 1], sgg[:, h * 128:(h + 1) * 128], ALU.mult, ALU.mult),
                    [bkey(4), "rsg", K_("sgg")], ["mixedb"], 0.35)
            P.cut()
            for k in range(8):
                pe(lambda e, k=k: e.transpose(bf(5)[:, k, :], mixed[:, k * 128:(k + 1) * 128], ident[:]), ["mixeda", "mixedb", "ident"], [bkey(5)], 0.1)
            act(lambda e: e.copy(mT, bf(5)), [bkey(5)], ["mT"], 0.9)
            hk = "hb%d" % slot
            pj_b = [0]
            for n in range(2):
                b = (6, 7)[n]
                for k in range(8):
                    pe(lambda e, k=k, b=b, n=n: e.matmul(bank[b][:], mT[:, k, :], wout[:, k, n * 512:(n + 1) * 512], start=(k == 0), stop=(k == 7)),
                       ["mT"] + WOUT, [bkey(b)])
                dve(lambda e, b=b, n=n: e.tensor_tensor(hb[slot][:, n * 512:(n + 1) * 512], x_t[:, n * 512:(n + 1) * 512], bank[b][:], ALU.add),
                    [bkey(b), xk], [hk], 0.69)
            rms_and_transpose(hb[slot][:], hk, 2, n2wT, "n2wT", hnT[:, :, slot * 128:(slot + 1) * 128], ["hnT%d" % slot], hn, "hn", 5)

        wu_i = [0]
        wd_i = [0]
        CAR = ["carry%d" % c for c in range(2 * NG)]

        pref = []

        def up_load(blk):
            s = wu_i[0] % NWU
            wu_i[0] += 1
            P.dma("sp", "wul%d_0" % s, wu[s][:, 0], wup_sv[:, :, blk * 256:(blk + 1) * 256], reads=WUPS, writes=["wu%d_0" % s])
            P.dma("sp", "wul%d_1" % s, wu[s][:, 1], wup_sv[:, :, DFF + blk * 256:DFF + (blk + 1) * 256], reads=WUPS, writes=["wu%d_1" % s])
            return s

        def prefetch_up():
            del pref[:]
            for blk in range(NWU):
                pref.append(up_load(blk))

        def ffn(ntl, out_tile0):
            halo = out_tile0 < 0
            ntok = ntl * 128 if not halo else 2
            tok0 = 0 if not halo else 126
            hkeys = ["hnT%d" % j for j in range(ntl)]
            if not halo:
                pool(lambda e: e.tensor_tensor(corr[:, :, 0], carry[:, :, 0], cwt[:, 0, :], ALU.mult), CAR + ["cwt0"], ["corr"])
                pool(lambda e: e.tensor_tensor(ctmp[:], carry[:, :, 1], cwt[:, 1, :], ALU.mult), CAR + ["cwt1"], ["ctmp"])
                pool(lambda e: e.tensor_tensor(corr[:, :, 0], corr[:, :, 0], ctmp[:], ALU.add), ["corr", "ctmp"], ["corr"])
                pool(lambda e: e.tensor_tensor(corr[:, :, 1], carry[:, :, 1], cwt[:, 0, :], ALU.mult), CAR + ["cwt0"], ["corr"])
            pair_i = 0
            for blk in range(NG // 2):
                if blk < len(pref):
                    s = pref[blk]
                else:
                    s = up_load(blk)
                for gi in range(2):
                    g = blk * 2 + gi
                    pp = pair_i % 3
                    pair_i += 1
                    bg, bv = 2 * pp, 2 * pp + 1
                    ag, av = acc[2 * pp], acc[2 * pp + 1]
                    kag, kav = "acc%d" % (2 * pp), "acc%d" % (2 * pp + 1)
                    for half, bb_ in ((0, bg), (1, bv)):
                        for k in range(8):
                            pe(lambda e, k=k, half=half, bb_=bb_, s=s, gi=gi: e.matmul(bank[bb_][:, 0:ntok], wu[s][:, half, k, gi * 128:(gi + 1) * 128], hnT[:, k, tok0:tok0 + ntok],
                                                                                     start=(k == 0), stop=(k == 7)), hkeys + ["wu%d_%d" % (s, half)], [bkey(bb_)], 0.27 if not halo else 0.07)
                    for (a, ka, bb_, ch) in ((ag, kag, bg, g), (av, kav, bv, NG + g)):
                        if not halo:
                            act(lambda e, a=a, bb_=bb_, ch=ch: e.activation(a[:, 0:ntok], bank[bb_][:, 0:ntok], AF.Identity, bias=cwt[:, 3, ch:ch + 1], scale=cwt[:, 2, ch:ch + 1]),
                                [bkey(bb_), "cwt2", "cwt3"], [ka], 0.62)
                            dve(lambda e, a=a, bb_=bb_, ch=ch: e.scalar_tensor_tensor(a[:, 1:ntok], bank[bb_][:, 0:ntok - 1], cwt[:, 1, ch:ch + 1], a[:, 1:ntok], ALU.mult, ALU.add),
                                [bkey(bb_), ka, "cwt1"], [ka], 0.7)
                            dve(lambda e, a=a, bb_=bb_, ch=ch: e.scalar_tensor_tensor(a[:, 2:ntok], bank[bb_][:, 0:ntok - 2], cwt[:, 0, ch:ch + 1], a[:, 2:ntok], ALU.mult, ALU.add),
                                [bkey(bb_), ka, "cwt0"], [ka], 0.7)
                            pool(lambda e, a=a, ch=ch: e.tensor_tensor(a[:, 0:2], a[:, 0:2], corr[:, ch, :], ALU.add), [ka, "corr"], [ka], 0.2)
                        act(lambda e, bb_=bb_, ch=ch: e.copy(carry[:, ch, :], bank[bb_][:, ntok - 2:ntok]), [bkey(bb_), "corr"], ["carry%d" % ch], 0.25)
                    if halo:
                        continue
                    act(lambda e, a=ag: e.activation(a[:, 0:ntok], a[:, 0:ntok], AF.Silu), [kag], [kag], 0.62)
                    pool(lambda e, g=g, a=ag, a2=av: e.tensor_tensor(actT[g][:, 0:ntok], a[:, 0:ntok], a2[:, 0:ntok], ALU.mult), [kag, kav], ["actT%d" % g], 1.26)
            del pref[:]
            if halo:
                prefetch_up()
                return
            if os.environ.get("KDEBUG") and out_tile0 == 12:
                print("ffn up done", {k: round(v, 1) for k, v in P.t_eng.items()})
            dbank = [6, 7, 0, 1]
            for n in range(2):
                for k0 in range(0, NG, 8):
                    nk = min(8, NG - k0)
                    s = wu_i[0] % NWU
                    wu_i[0] += 1
                    wv = wu[s][:].rearrange("p a k c -> p (a k c)").rearrange("p (k c) -> p k c", c=512)
                    P.dma("sp", "wul%d_0" % s, wv[:, 0:nk, :], wdn_sv[:, k0:k0 + nk, n * 512:(n + 1) * 512], reads=WDNS, writes=["wu%d_0" % s, "wu%d_1" % s])
                    for kk in range(nk):
                        k = k0 + kk
                        for j in range(ntl):
                            pe(lambda e, k=k, kk=kk, j=j, wv=wv: e.matmul(bank[dbank[j]][:], actT[k][:, j * 128:(j + 1) * 128], wv[:, kk, :],
                                                                         start=(k == 0), stop=(k == NG - 1)), ["actT%d" % k, "wu%d_0" % s, "wu%d_1" % s], [bkey(dbank[j])])
                for j in range(ntl):
                    dve(lambda e, j=j, n=n: e.tensor_tensor(hb[j][:, n * 512:(n + 1) * 512], hb[j][:, n * 512:(n + 1) * 512], bank[dbank[j]][:], ALU.add),
                        [bkey(dbank[j]), "hb%d" % j], ["hb%d" % j], 0.69)
            if os.environ.get("KDEBUG") and out_tile0 == 12:
                print("ffn down done", {k: round(v, 1) for k, v in P.t_eng.items()})
            if out_tile0 + ntl < NMAIN:
                prefetch_up()
            P.record()
            for j in range(ntl):
                hk = "hb%d" % j
                ss = sm[:, 4:5]
                rs = sm[:, 5:6]
                act(lambda e, j=j: e.activation(junk, hb[j][:], AF.Square, accum_out=ss), [hk], ["ss4", "junk"], 1.1)
                pool(lambda e: e.tensor_scalar(rs, ss, 1.0 / D, EPS, ALU.mult, ALU.add), ["ss4"], ["rs4"], 0.2)
                pool(lambda e: e.tensor_tensor(rs, rs, cvec[:, 12:13], ALU.pow), ["rs4", "cvec"], ["rs4"], 0.5)
                dve(lambda e, j=j: e.scalar_tensor_tensor(hb[j][:], hb[j][:], rs, fnw[:], ALU.mult, ALU.mult), [hk, "rs4", "fnw"], [hk], 1.3)
                r0 = (out_tile0 + j) * 128
                P.dma("sp", "yst%d" % j, y_d[r0:r0 + 128, :], hb[j][:], reads=[hk])
            return P.stop()

        tiles = [(t, False, 0, None) for t in range(NPRE)]
        tiles.append((NPRE, True, 0, (1, -1)))
        for s in range(NSUP):
            for j in range(NT):
                tiles.append((NPRE + 1 + s * NT + j, True, j, (NT, s * NT) if j == NT - 1 else None))

        def rec_a(i):
            t, full, slot, f = tiles[i]
            P.record()
            if t + 1 < NV:
                load_x(t + 1)
            stage_a(t, full)
            return P.stop()

        def rec_b(i):
            t, full, slot, f = tiles[i]
            P.record()
            stage_b(t, full, slot)
            return P.stop()

        def stg_a(i):
            t, full, slot, f = tiles[i]
            P.record()
            if t + 1 < NV:
                load_x(t + 1)
            stage_a(t, full)
            c = P.stop_chunks()
            if full or not SPLIT:
                return [], [], [[c[0]], [c[1], c[2]]]
            return [[c[0]]], [[c[1]]], [[c[2]]]

        def stg_b(i):
            t, full, slot, f = tiles[i]
            P.record()
            stage_b(t, full, slot)
            c = P.stop_chunks()
            if not full:
                return [[c[0]], [c[1]]]
            return [[c[0]], [c[1], c[2]], [c[3]], [c[4]]]

        PIPE = int(os.environ.get("PIPE", "3"))
        SPLIT = bool(int(os.environ.get("SPLIT", "1")))
        FILL = bool(int(os.environ.get("FILL", "0")))
        assert NPRE % 2 == 1
        tails = {}
        BULK_PER = (len(bulk) + max(1, NPRE // 2) - 1) // max(1, NPRE // 2)
        stA = {}
        pend = [None]
        if PIPE == 3:
            stA[0] = stg_a(0)
            stA[1] = stg_a(1)
            if stA[0][0]:
                P.play_stages([stA[0][0]])
            P.play_stages([st_ for st_ in (stA[0][1], stA[1][0]) if st_])
        else:
            P.play(rec_a(0))
        for i in range(len(tiles) + (1 if PIPE == 3 else 0)):
            if PIPE != 3:
                ra = rec_a(i + 1) if i + 1 < len(tiles) else []
                rb = rec_b(i)
            if PIPE == 3:
                P.filler = filler if (FILL and i < NPRE - 1) else None
                issue_bulk(BULK_PER if i < NPRE - 2 else len(bulk))
                if i == NPRE - 2:
                    prefetch_up()
                if i + 2 < len(tiles):
                    stA[i + 2] = stg_a(i + 2)
                h1 = stA[i + 2][0] if i + 2 in stA else []
                h2 = stA[i + 1][1] if i + 1 in stA else []
                tl = stA[i][2] if i in stA else []
                sb_ = stg_b(i - 1) if i >= 1 else []
                if pend[0]:
                    P.play_stages([st_ for st_ in (sb_[:-1], tl, h2, h1, [[pend[0]]]) if st_])
                    P.play_stages([[sb_[-1]]] if sb_ else [])
                    pend[0] = None
                else:
                    P.play_stages([st_ for st_ in (sb_, tl, h2, h1) if st_])
                stA.pop(i, None)
                if os.environ.get("KDEBUG") and (i in (8, 9, 10, 11) or 44 <= i <= 46):
                    print("after step", i, {k: round(v, 1) for k, v in P.t_eng.items()})
                if i >= 1 and tiles[i - 1][3] is not None:
                    pend[0] = ffn(*tiles[i - 1][3])
                continue
            if False:
                if os.environ.get("KDEBUG") and (44 <= i <= 48 or i in (5, 6, 7)):
                    print("after tile", i, {k: round(v, 1) for k, v in P.t_eng.items()})
                if tiles[i][3] is not None:
                    ffn(*tiles[i][3])
                continue
            if PIPE == 2:
                P.play_merged(rb, ra)
            elif PIPE == 1:
                P.play(interleave(ra, rb))
            else:
                P.play(rb + ra)
            if os.environ.get("KDEBUG") and 44 <= i <= 53:
                print("after tile", i, {k: round(v, 1) for k, v in P.t_eng.items()})
            if tiles[i][3] is not None:
                ffn(*tiles[i][3])
                if os.environ.get("KDEBUG") and 44 <= i <= 53:
                    print("after ffn", i, {k: round(v, 1) for k, v in P.t_eng.items()})
        if pend[0]:
            P.play(pend[0])
        for j in range(NT):
            ent = P.dma_sems["yst%d" % j]
            P.streams["sp"].append(("wait", ent[0], ent[1]))
        if os.environ.get("KDEBUG"):
            print("fillers:", P.n_fill)
            print("MODEL est total us:", round(max(P.t_eng.values()), 1), {k: round(v, 1) for k, v in P.t_eng.items()}, "n_ops", P.cnt)
        P.emit()
    return nc


def _consts():
    l = np.arange(128, dtype=np.float64)
    c = {}
    c["c_ident"] = np.eye(128, dtype=np.float32)
    le = (l[:, None] <= l[None, :])
    c["c_maskr"] = (le / math.sqrt(128.0)).astype(np.float32)
    c["c_maskg"] = le.astype(np.float32)
    c["c_triu"] = le.astype(np.float32)
    c["c_tril"] = (l[:, None] > l[None, :]).astype(np.float32)
    invf = 10000.0 ** (-np.arange(64, dtype=np.float64) / 64.0)
    rot = np.zeros((2, 128), np.float64)
    rot[0, :64] = invf / (2 * math.pi)
    rot[0, 64:] = invf / (2 * math.pi)
    rot[1, 64:] = 0.25
    c["c_rot"] = rot.astype(np.float32)
    cv = np.zeros((128, 16), np.float64)
    for h in range(4):
        cv[:, h] = GAM[h] ** (-(l + 1))
        cv[:, 4 + h] = GAM[h] ** (l + 1)
        cv[:, 8 + h] = GAM[h] ** (2 * (l + 1))
    cv[:, 12] = -0.5
    cv[:, 13] = 1.0
    c["c_vec"] = cv.astype(np.float32)
    return c


_NC_CACHE = {}


def _run(inputs, NPRE, NSUP, seq):
    x = np.asarray(inputs["x"], dtype=np.float32)
    pos = np.asarray(inputs["positions"], dtype=np.int32)
    B = x.shape[0]
    M = NT * NSUP * 128
    NV = NPRE + 1 + NT * NSUP
    assert seq == 2 * M and (NPRE + 1) * 128 == M
    key = (NPRE, NSUP)
    if key not in _NC_CACHE:
        _NC_CACHE[key] = build(NPRE, NSUP)
    nc = _NC_CACHE[key]
    consts = _consts()
    shared = {
        "norm1_w": np.ascontiguousarray(inputs["norm1_w"][0], np.float32),
        "norm2_w": np.ascontiguousarray(inputs["norm2_w"][0], np.float32),
        "final_norm_w": np.ascontiguousarray(inputs["final_norm_w"], np.float32).reshape(1, D),
        "w_in": np.ascontiguousarray(inputs["w_in"][0], np.float32),
        "w_out": np.ascontiguousarray(inputs["w_out"][0], np.float32),
        "ffn_w_up": np.ascontiguousarray(inputs["ffn_w_up"][0], np.float32),
        "ffn_w_down": np.ascontiguousarray(inputs["ffn_w_down"][0], np.float32),
        "ret_norm_w": np.ascontiguousarray(inputs["ret_norm_w"][0], np.float32).reshape(1, 512),
        "ret_norm_b": np.ascontiguousarray(inputs["ret_norm_b"][0], np.float32).reshape(1, 512),
        "gla_norm_w": np.ascontiguousarray(inputs["gla_norm_w"][0], np.float32).reshape(1, 512),
        "gla_gate_w2": np.ascontiguousarray(inputs["gla_gate_w2"][0], np.float32),
        "gla_gate_b": np.ascontiguousarray(inputs["gla_gate_b"][0], np.float32).reshape(1, 256),
        "ffn_conv_w": np.ascontiguousarray(inputs["ffn_conv_w"][0], np.float32),
        "ffn_conv_b": np.ascontiguousarray(inputs["ffn_conv_b"][0], np.float32),
    }
    shared.update(consts)
    in_maps = []
    for core in range(2 * B):
        b, half = core // 2, core % 2
        if half == 0:
            xs = np.concatenate([np.zeros((M, D), np.float32), x[b, :M]], axis=0)
            ps = np.concatenate([np.zeros((M,), np.int32), pos[b, :M]], axis=0)
        else:
            xs = x[b]
            ps = pos[b]
        m = dict(shared)
        m["xs"] = np.ascontiguousarray(xs)
        m["pos"] = np.ascontiguousarray(ps.reshape(NV, 128))
        in_maps.append(m)
    res = run_bass_kernel_spmd(nc, in_maps, core_ids=list(range(2 * B)))
    out = np.empty((B, seq, D), np.float32)
    for core in range(2 * B):
        b, half = core // 2, core % 2
        out[b, half * M:(half + 1) * M] = res.results[core]["y"]
    return out


def kernel(**inputs):
    return _run(inputs, NPRE=31, NSUP=8, seq=8192)
```
